# Optimizing a Trainium2 kernel written in Bass

```python
import math
import jax, jax.numpy as jnp
from jax import lax
import numpy as np

D_MODEL = 1024
BATCH = 2
SEQ = 16384
DEPTH = 2

HEAD_DIM = 64
GRID_W = 64
EPS = 1e-6

ATT_HEADS = 8
ATT_KV_HEADS = 2
ATT_WIDTH = ATT_HEADS * HEAD_DIM
KV_WIDTH = ATT_KV_HEADS * HEAD_DIM
Q_BLOCK = 128
ROPE_THETA = 10000.0
ROPE_AXIS_DIM = HEAD_DIM // 2

CONV_GROUPS = 4
CONV_WIDTH = CONV_GROUPS * HEAD_DIM
CONV_KERNEL = 31

SG_HEADS = 4
SG_WIDTH = SG_HEADS * HEAD_DIM
SG_CHUNK = 128

D_MIX = ATT_WIDTH + CONV_WIDTH + SG_WIDTH

IN_SPLIT_SIZES = (
    ATT_WIDTH,
    KV_WIDTH,
    KV_WIDTH,
    ATT_WIDTH,
    2 * CONV_WIDTH,
    CONV_WIDTH,
    SG_WIDTH,
    SG_WIDTH,
    SG_WIDTH,
)
D_IN = sum(IN_SPLIT_SIZES)

kernel_name = "hybrid_parallel_conv_gqa_sgu_encoder"


def _split_points():
    pts, acc = [], 0
    for s in IN_SPLIT_SIZES[:-1]:
        acc += s
        pts.append(acc)
    return pts


def rms_norm(x, g):
    xf = x.astype(jnp.float32)
    y = xf * lax.rsqrt(jnp.mean(xf * xf, axis=-1, keepdims=True) + EPS) * g.astype(jnp.float32)
    return y.astype(x.dtype)


def layer_norm(x, g, b):
    xf = x.astype(jnp.float32)
    mu = jnp.mean(xf, axis=-1, keepdims=True)
    xc = xf - mu
    var = jnp.mean(xc * xc, axis=-1, keepdims=True)
    y = xc * lax.rsqrt(var + EPS) * g.astype(jnp.float32) + b.astype(jnp.float32)
    return y.astype(x.dtype)


def rope_1d(x, pos):
    d = x.shape[-1]
    half = d // 2
    inv_freq = ROPE_THETA ** (-jnp.arange(half, dtype=jnp.float32) / half)
    ang = pos[:, None] * inv_freq[None, :]
    cos = jnp.cos(ang)[:, None, :]
    sin = jnp.sin(ang)[:, None, :]
    xf = x.astype(jnp.float32)
    x1, x2 = xf[..., :half], xf[..., half:]
    out = jnp.concatenate([x1 * cos - x2 * sin, x2 * cos + x1 * sin], axis=-1)
    return out.astype(x.dtype)


def axial_rope(x, row, col):
    return jnp.concatenate([rope_1d(x[..., :ROPE_AXIS_DIM], row),
                            rope_1d(x[..., ROPE_AXIS_DIM:], col)], axis=-1)


def attention_group(q, k, v):
    B, S = q.shape[0], q.shape[1]
    G = ATT_HEADS // ATT_KV_HEADS
    nblk = S // Q_BLOCK
    qb = q.reshape(B, nblk, Q_BLOCK, ATT_KV_HEADS, G, HEAD_DIM).transpose(1, 0, 3, 4, 2, 5)
    kt = k.transpose(0, 2, 1, 3)
    vt = v.transpose(0, 2, 1, 3)
    scale = HEAD_DIM ** -0.5

    def one_block(qi):
        s = jnp.einsum('bkgqd,bksd->bkgqs', qi, kt, preferred_element_type=jnp.float32) * scale
        p = jax.nn.softmax(s, axis=-1)
        return jnp.einsum('bkgqs,bksd->bkgqd', p.astype(vt.dtype), vt)

    o = lax.map(one_block, qb)
    return o.transpose(1, 0, 4, 2, 3, 5).reshape(B, S, ATT_WIDTH)


def conv_group(a, dw_w, dw_b, ln_g, ln_b):
    h = a[..., :CONV_WIDTH] * jax.nn.sigmoid(a[..., CONV_WIDTH:])
    pad = CONV_KERNEL // 2
    h = lax.conv_general_dilated(
        h, dw_w[:, None, :].astype(h.dtype), window_strides=(1,), padding=[(pad, pad)],
        dimension_numbers=('NWC', 'WIO', 'NWC'), feature_group_count=CONV_WIDTH) + dw_b
    h = layer_norm(h, ln_g, ln_b)
    return jax.nn.silu(h)


def spatial_gating_group(u, v, ln_g, ln_b, w_s, b_s):
    B, S = u.shape[0], u.shape[1]
    u = jax.nn.gelu(u, approximate=False)
    v = layer_norm(jax.nn.gelu(v, approximate=False), ln_g, ln_b)
    n = S // SG_CHUNK
    vc = v.reshape(B, n, SG_CHUNK, SG_HEADS, HEAD_DIM)
    mixed = jnp.einsum('hpq,bnqhd->bnphd', w_s, vc) + b_s.T[None, None, :, :, None]
    return u * mixed.reshape(B, S, SG_WIDTH)


def setup_inputs(seed: int = 0) -> dict:
    key = jax.random.key(seed)
    ks = jax.random.split(key, 16)
    f32 = jnp.float32
    x = jax.random.normal(ks[0], (BATCH, SEQ, D_MODEL), f32)
    pre_norm = 1.0 + 0.05 * jax.random.normal(ks[1], (DEPTH, D_MODEL), f32)
    post_norm = 1.0 + 0.05 * jax.random.normal(ks[2], (DEPTH, D_MODEL), f32)
    w_in = jax.random.normal(ks[3], (DEPTH, D_MODEL, D_IN), f32) * D_MODEL ** -0.5
    w_out = jax.random.normal(ks[4], (DEPTH, D_MIX, D_MODEL), f32) * D_MIX ** -0.5
    q_norm = 1.0 + 0.05 * jax.random.normal(ks[5], (DEPTH, HEAD_DIM), f32)
    k_norm = 1.0 + 0.05 * jax.random.normal(ks[6], (DEPTH, HEAD_DIM), f32)
    conv_dw = jax.random.normal(ks[7], (DEPTH, CONV_KERNEL, CONV_WIDTH), f32) * CONV_KERNEL ** -0.5
    conv_dw_b = 0.02 * jax.random.normal(ks[8], (DEPTH, CONV_WIDTH), f32)
    conv_ln_g = 1.0 + 0.05 * jax.random.normal(ks[9], (DEPTH, CONV_WIDTH), f32)
    conv_ln_b = 0.02 * jax.random.normal(ks[10], (DEPTH, CONV_WIDTH), f32)
    sg_ln_g = 1.0 + 0.05 * jax.random.normal(ks[11], (DEPTH, SG_WIDTH), f32)
    sg_ln_b = 0.02 * jax.random.normal(ks[12], (DEPTH, SG_WIDTH), f32)
    sg_w = jax.random.normal(ks[13], (DEPTH, SG_HEADS, SG_CHUNK, SG_CHUNK), f32) * SG_CHUNK ** -0.5
    sg_b = 1.0 + 0.1 * jax.random.normal(ks[14], (DEPTH, SG_HEADS, SG_CHUNK), f32)
    return {"x": x, "pre_norm": pre_norm, "post_norm": post_norm, "w_in": w_in, "w_out": w_out,
            "q_norm": q_norm, "k_norm": k_norm, "conv_dw": conv_dw, "conv_dw_b": conv_dw_b,
            "conv_ln_g": conv_ln_g, "conv_ln_b": conv_ln_b, "sg_ln_g": sg_ln_g, "sg_ln_b": sg_ln_b,
            "sg_w": sg_w, "sg_b": sg_b}


def reference(x, pre_norm, post_norm, w_in, w_out, q_norm, k_norm, conv_dw, conv_dw_b,
              conv_ln_g, conv_ln_b, sg_ln_g, sg_ln_b, sg_w, sg_b):
    B, S = x.shape[0], x.shape[1]
    rows = S // GRID_W
    row = jnp.repeat(jnp.arange(rows, dtype=jnp.int32), GRID_W).astype(jnp.float32)
    col = jnp.tile(jnp.arange(GRID_W, dtype=jnp.int32), rows).astype(jnp.float32)
    split_pts = _split_points()

    for l in range(DEPTH):
        h = rms_norm(x, pre_norm[l])
        proj = jnp.einsum('bsd,de->bse', h, w_in[l])
        q, k, v, g_att, a_conv, g_conv, u_sg, v_sg, g_sg = jnp.split(proj, split_pts, axis=-1)

        q = axial_rope(rms_norm(q.reshape(B, S, ATT_HEADS, HEAD_DIM), q_norm[l]), row, col)
        k = axial_rope(rms_norm(k.reshape(B, S, ATT_KV_HEADS, HEAD_DIM), k_norm[l]), row, col)
        v = v.reshape(B, S, ATT_KV_HEADS, HEAD_DIM)
        att = attention_group(q, k, v) * jax.nn.silu(g_att)

        cnv = conv_group(a_conv, conv_dw[l], conv_dw_b[l], conv_ln_g[l], conv_ln_b[l]) * jax.nn.silu(g_conv)

        sgu = spatial_gating_group(u_sg, v_sg, sg_ln_g[l], sg_ln_b[l], sg_w[l], sg_b[l]) * jax.nn.silu(g_sg)

        mix = jnp.einsum('bse,ed->bsd', jnp.concatenate([att, cnv, sgu], axis=-1), w_out[l])
        x = x + rms_norm(mix, post_norm[l])
    return x
```

```python
import numpy as np
import ml_dtypes
import concourse.bass as bass
import concourse.mybir as mybir
from concourse.bass_utils import run_bass_kernel_spmd

F32 = mybir.dt.float32
BF16 = mybir.dt.bfloat16
AF = mybir.ActivationFunctionType
ALU = mybir.AluOpType

D = 1024
DIN = 2816
NCORES = 8
EPS = 1e-6
HALO = 16

QC, KC, VC, GA, CA, CB, GC, SG = 0, 512, 640, 768, 1280, 1536, 1792, 2048
P_GPRE, P_GQ, P_GK, P_EPS, P_CB, P_CLG, P_CLB, P_DW, P_N = 0, 8, 9, 10, 11, 13, 15, 17, 17 + 62
B_POST, B_SLG, B_SLB, B_SGB, B_N = 0, 1024, 1280, 1536, 1540


class Dep:
    __slots__ = ("lw", "rd")

    def __init__(self):
        self.lw = None
        self.rd = {}


class Buf:
    __slots__ = ("t", "d", "name", "psum")

    def __init__(self, t, name="", d=None, psum=False):
        self.t = t
        self.d = d if d is not None else Dep()
        self.name = name
        self.psum = psum

    def __getitem__(self, idx):
        return self.t[idx]

    def alias(self, ap, name=""):
        return Buf(ap, name, self.d, self.psum)


class K:
    def __init__(self, nc, same_engine_sync=True):
        self.nc = nc
        self.eng = {"pe": nc.tensor, "act": nc.scalar, "dve": nc.vector, "pool": nc.gpsimd, "sp": nc.sync}
        self.sem = {e: nc.alloc_semaphore("prog_" + e) for e in self.eng}
        self.cnt = {e: 0 for e in self.eng}
        self.seen = {e: {} for e in self.eng}
        self.same = same_engine_sync
        self.nbuf = 0
        self.dsems = {}
        self.dcnt = {}

    def sb(self, shape, dt, name=None):
        self.nbuf += 1
        name = "s_" + (name or f"sb{self.nbuf}")
        return Buf(self.nc.alloc_sbuf_tensor(name, list(shape), dt), name)

    def _deps(self, e, reads, writes):
        need = {}
        own = self.sem[e]
        for b in reads:
            if b.d.lw is not None:
                s, v = b.d.lw
                if need.get(s, 0) < v:
                    need[s] = v
            if b.psum:
                for s, v in b.d.rd.items():
                    if s is not own and need.get(s, 0) < v:
                        need[s] = v
        for b in writes:
            if b.d.lw is not None:
                s, v = b.d.lw
                if need.get(s, 0) < v:
                    need[s] = v
            for s, v in b.d.rd.items():
                if need.get(s, 0) < v:
                    need[s] = v
        own = self.sem[e]
        seen = self.seen[e]
        for s, v in need.items():
            if s is own and (e == "pe" or not self.same):
                continue
            if seen.get(s, 0) >= v:
                continue
            self.eng[e].wait_ge(s, v)
            seen[s] = v

    def _done(self, tok, reads, writes):
        s, v = tok
        for b in writes:
            b.d.lw = tok
            b.d.rd = {}
        for b in reads:
            if b.d.rd.get(s, 0) < v:
                b.d.rd[s] = v

    def op(self, e, fn, reads=(), writes=()):
        self._deps(e, reads, writes)
        ins = fn(self.eng[e])
        self.cnt[e] += 1
        ins.then_inc(self.sem[e], 1)
        self._done((self.sem[e], self.cnt[e]), reads, writes)
        return ins

    def dma(self, q, semname, pairs, reads=(), writes=()):
        if semname not in self.dsems:
            self.dsems[semname] = self.nc.alloc_semaphore("dma_" + semname)
            self.dcnt[semname] = 0
        s = self.dsems[semname]
        self._deps(q, reads, writes)
        for o, i in pairs:
            self.eng[q].dma_start(out=o, in_=i).then_inc(s, 16)
            self.dcnt[semname] += 16
        self._done((s, self.dcnt[semname]), reads, writes)

    def wait_all(self, e, bufs):
        self._deps(e, bufs, [])


class _Stop(Exception):
    pass


def build_layer(TOWN, SKV, dbg=None, stop=None):
    NT = TOWN // 512
    NKT = SKV // 512
    NK = SKV // 128
    nc = bass.Bass("TRN2", target_bir_lowering=False)
    k = K(nc)

    def dram(name, shape, dt=F32, kind="ExternalInput"):
        return nc.dram_tensor(name, list(shape), dt, kind=kind).ap()

    x_all = dram("x_all", [SKV, D])
    x_own = dram("x_own", [TOWN + 2 * HALO, D])
    w_in = dram("w_in", [D, DIN])
    w_out = dram("w_out", [D, D])
    ropek = dram("ropek", [2, 128, SKV])
    ropeq = dram("ropeq", [2, 128, TOWN])
    cbf_d = dram("cbf", [128, 4, 128], BF16)
    prm_d = dram("prm", [128, P_N])
    bc_d = dram("bc", [128, B_N])
    wst_d = dram("wst", [128, 4, 128])
    y_d = dram("y", [TOWN, D], kind="ExternalOutput")
    dbg_d = {}
    if dbg:
        for nm, shp in dbg.items():
            dbg_d[nm] = dram("dbg_" + nm, shp, kind="ExternalOutput")

    KT = k.sb([128, SKV], BF16, "KT")
    VA = k.sb([128, NK, 2, 66], BF16, "VA")
    wbf = k.sb([128, 8, DIN], BF16, "wbf")
    wobf = k.sb([128, 8, D], BF16, "wobf")
    wsbf = k.sb([128, 4, 128], BF16, "wsbf")
    cbf = k.sb([128, 4, 128], BF16, "cbf")
    prm = k.sb([128, P_N], F32, "prm")
    bc = k.sb([128, B_N], F32, "bc")
    xs = [k.sb([128, D], F32, f"xs{i}") for i in range(2)]
    xbs = [k.sb([128, D], BF16, f"xb{i}") for i in range(2)]
    yb = k.sb([128, D], F32, "yb")
    hT = k.sb([128, 8, 544], BF16, "hT")
    cosb = k.sb([128, 512], F32, "cosb")
    sinb = k.sb([128, 512], F32, "sinb")
    tA = k.sb([128, 512], F32, "tA")
    tB = k.sb([128, 512], F32, "tB")
    tC = k.sb([128, 512], F32, "tC")
    sqb = k.sb([128, 512], BF16, "sqb")
    qgb = k.sb([128, 512], BF16, "qgb")
    QT = k.sb([128, 4, 512], BF16, "QT")
    gatt = k.sb([128, 4, 512], BF16, "gatt")
    hpad = k.sb([128, 2, 544], F32, "hpad")
    acc = k.sb([128, 2, 512], F32, "acc")
    gconv = k.sb([128, 2, 512], BF16, "gconv")
    vln = k.sb([128, 256], BF16, "vln")
    mixT = k.sb([128, 8, 512], BF16, "mixT")
    Pb = [k.sb([128, 1024], BF16, f"P{i}") for i in range(2)]
    st = k.sb([128, 32], F32, "st")
    cb16 = Pb[0].alias(Pb[0][:, :].rearrange("p (a b) -> p a b", a=2), "cb16")
    csq16 = Pb[1].alias(Pb[1][:, :].rearrange("p (a b) -> p a b", a=2), "csq16")
    sgt = gconv.alias(gconv[:, :, :].rearrange("p a (c d) -> p (a c) d", c=2), "sgt")
    Osb = acc.alias(acc[:, :, :].rearrange("p a b -> p (a b)"), "Osb")
    gu = tA.alias(tA[:, 0:256], "gu")
    gv = tA.alias(tA[:, 256:512], "gv")
    gg = tB.alias(tB[:, 0:256], "gg")
    tmpA = qgb
    tmpB = sqb
    onesf = k.sb([128, 64], F32, "onesf")

    S0t = nc.alloc_psum_tensor("S0", [128, 1024], F32)
    S1t = nc.alloc_psum_tensor("S1", [128, 1024], F32)
    OEt = nc.alloc_psum_tensor("OE", [128, 512], F32)
    OOt = nc.alloc_psum_tensor("OO", [128, 512], F32)
    pT = [Buf(nc.alloc_psum_tensor(f"pT{i}", [128, 1024], BF16), f"pT{i}", psum=True) for i in range(2)]
    S0 = [Buf(S0t[:, 0:512], "S0a", psum=True), Buf(S0t[:, 512:1024], "S0b", psum=True)]
    S1 = [Buf(S1t[:, 0:512], "S1a", psum=True), Buf(S1t[:, 512:1024], "S1b", psum=True)]
    OE = Buf(OEt[:, :], "OE", psum=True)
    OO = Buf(OOt[:, :], "OO", psum=True)
    pBc = [pT[i].alias(pT[i][:, :].bitcast(F32), f"pBc{i}") for i in range(2)]
    banks = [OE, OO, S0[0], S0[1], S1[0], S1[1]]
    bank_i = [0]

    def bank():
        b = banks[bank_i[0] % len(banks)]
        bank_i[0] += 1
        return b

    ident = cbf[:, 0, :]
    RT = cbf[:, 1, :]
    bones = cbf[:, 2, :]
    o256 = cbf[:, 3, :]

    def pcol(c, rows=128):
        return prm[0:rows, c:c + 1]

    def ck(name):
        if stop == name:
            raise _Stop()

    def body():
        k.dma("sp", "c0", [(cbf[:], cbf_d[:, :, :]), (prm[:], prm_d[:, :]), (bc[:], bc_d[:, :])], writes=[cbf, prm, bc])
        k.op("pool", lambda e: e.memset(VA[:], 1.0), writes=[VA])
        k.op("dve", lambda e: e.memset(st[:], 0.0), writes=[st])
        k.op("dve", lambda e: e.memset(onesf[:], 1.0), writes=[onesf])
        k.dma("sp", "xs0", [(xs[0][:, 0:512], wst_d.rearrange("q h p -> q (h p)"))], writes=[xs[0]])
        k.op("dve", lambda e: e.tensor_copy(wsbf[:].rearrange("q h p -> q (h p)"), xs[0][:, 0:512]), reads=[xs[0]], writes=[wsbf])
        si = 1
        for kc in range(8):
            for (c0, c1) in ((0, 1024), (1024, 2048), (2048, DIN)):
                s = xs[si % 2]
                k.dma("sp", f"xs{si % 2}", [(s[:, 0:c1 - c0], w_in[kc * 128:(kc + 1) * 128, c0:c1])], writes=[s])
                e = "dve" if si % 2 else "pool"
                k.op(e, lambda en, s=s, kc=kc, c0=c0, c1=c1: en.tensor_scalar(
                    wbf[:, kc, c0:c1], s[:, 0:c1 - c0], pcol(P_GPRE + kc), None, op0=ALU.mult), reads=[s, prm], writes=[wbf])
                si += 1
        for kc in range(8):
            s = xs[si % 2]
            k.dma("sp", f"xs{si % 2}", [(s[:], w_out[kc * 128:(kc + 1) * 128, :])], writes=[s])
            e = "dve" if si % 2 else "pool"
            k.op(e, lambda en, s=s, kc=kc: en.tensor_copy(wobf[:, kc, :], s[:]), reads=[s], writes=[wobf])
            si += 1
        xslot = [si]

        def load_norm_T(src, row0, nrows, col0):
            i = xslot[0]
            xslot[0] += 1
            s, xb, pt = xs[i % 2], xbs[i % 2], pT[i % 2]
            c = i % 8
            k.dma("sp", f"xs{i % 2}", [(s[0:nrows, :], src[row0:row0 + nrows, :])], writes=[s])
            k.op("act", lambda e: e.activation(yb[0:nrows, :], s[0:nrows, :], AF.Square, accum_out=st[0:nrows, c:c + 1]),
                 reads=[s], writes=[yb, st])
            k.op("dve", lambda e: e.tensor_scalar(st[0:nrows, 8 + c:9 + c], st[0:nrows, c:c + 1], 1.0 / D, EPS,
                                                  op0=ALU.mult, op1=ALU.add), reads=[st], writes=[st])
            k.op("act", lambda e: e.activation(st[0:nrows, 16 + c:17 + c], st[0:nrows, 8 + c:9 + c], AF.Sqrt), reads=[st], writes=[st])
            k.op("dve", lambda e: e.reciprocal(st[0:nrows, 24 + c:25 + c], st[0:nrows, 16 + c:17 + c]), reads=[st], writes=[st])
            k.op("dve", lambda e: e.tensor_scalar(xb[0:nrows, :], s[0:nrows, :], st[0:nrows, 24 + c:25 + c], None, op0=ALU.mult),
                 reads=[s, st], writes=[xb])
            for kc in range(8):
                k.op("pe", lambda e, kc=kc: e.transpose(pt[:, kc * 128:kc * 128 + nrows], xb[0:nrows, kc * 128:(kc + 1) * 128],
                                                        ident[0:nrows, 0:nrows]), reads=[xb, cbf], writes=[pt])
            src_ap = pt[:, :].rearrange("p (a b) -> p a b", a=8)[:, :, 0:nrows]
            k.op("act", lambda e: e.activation(hT[:, :, col0:col0 + nrows], src_ap, AF.Copy), reads=[pt], writes=[hT])

        def proj_fm(col0, n0=0, n=512, m=128):
            b = bank()
            for kc in range(8):
                k.op("pe", lambda e, kc=kc: e.matmul(b[0:m, 0:n], lhsT=wbf[:, kc, col0:col0 + m], rhs=hT[:, kc, n0:n0 + n],
                                                     start=(kc == 0), stop=(kc == 7)), reads=[wbf, hT], writes=[b])
            return b

        def norm_rope(pP, gcol, out_ap, out_buf):
            k.op("act", lambda e: e.activation(sqb[:], pP[:, :], AF.Square), reads=[pP], writes=[sqb])
            k.op("dve", lambda e: e.tensor_scalar(qgb[:], pP[:, :], pcol(gcol), None, op0=ALU.mult), reads=[pP, prm], writes=[qgb])
            k.op("dve", lambda e: e.scalar_tensor_tensor(tA[:], pP[:, :], pcol(gcol), cosb[:], op0=ALU.mult, op1=ALU.mult),
                 reads=[pP, prm, cosb], writes=[tA])
            pMS = bank()
            k.op("pe", lambda e: e.matmul(pMS[:, :], lhsT=bones, rhs=sqb[:], start=True, stop=True), reads=[cbf, sqb], writes=[pMS])
            pRO = bank()
            k.op("pe", lambda e: e.matmul(pRO[:, :], lhsT=RT, rhs=qgb[:], start=True, stop=True), reads=[cbf, qgb], writes=[pRO])
            k.op("act", lambda e: e.activation(tB[:], pMS[:, :], AF.Sqrt, bias=pcol(P_EPS)), reads=[pMS, prm], writes=[tB])
            k.op("dve", lambda e: e.reciprocal(tB[:], tB[:]), reads=[tB], writes=[tB])
            k.op("dve", lambda e: e.tensor_tensor(tC[:], pRO[:, :], sinb[:], op=ALU.mult), reads=[pRO, sinb], writes=[tC])
            k.op("dve", lambda e: e.tensor_tensor(tA[:], tA[:], tC[:], op=ALU.add), reads=[tA, tC], writes=[tA])
            k.op("dve", lambda e: e.tensor_tensor(out_ap, tA[:], tB[:], op=ALU.mult), reads=[tA, tB], writes=[out_buf])

        def load_rope(tab, t0):
            k.dma("sp", "rope", [(cosb[:], tab[0, :, t0:t0 + 512]), (sinb[:], tab[1, :, t0:t0 + 512])], writes=[cosb, sinb])

        ck("setup")
        for j in range(NKT):
            load_rope(ropek, j * 512)
            ck("k_rope")
            for blk in range(4):
                load_norm_T(x_all, j * 512 + blk * 128, 128, blk * 128)
                ck("k_lnt1")
            pK = proj_fm(KC)
            ck("k_proj")
            norm_rope(pK, P_GK, KT[:, j * 512:(j + 1) * 512], KT)
            ck("k_nr")
            pV = bank()
            for blk in range(4):
                for kc in range(8):
                    k.op("pe", lambda e, kc=kc, blk=blk: e.matmul(pV[:, blk * 128:(blk + 1) * 128], lhsT=hT[:, kc, blk * 128:(blk + 1) * 128],
                                                                  rhs=wbf[:, kc, VC:VC + 128], start=(kc == 0), stop=(kc == 7)),
                         reads=[wbf, hT], writes=[pV])
            ck("k_vmm")
            k.op("act", lambda e: e.activation(VA[:, j * 4:(j + 1) * 4, :, 0:64],
                                               pV[:, :].rearrange("p (a g d) -> p a g d", a=4, g=2), AF.Copy), reads=[pV], writes=[VA])

        if dbg and "KT" in dbg:
            k.op("dve", lambda e: e.tensor_copy(tA[:], KT[:, 0:512]), reads=[KT], writes=[tA])
            k.dma("sp", "dbg", [(dbg_d["KT"][:, :], tA[:])], reads=[tA])

        ck("phaseK")
        for i in range(NT):
            r0 = i * 512
            load_rope(ropeq, i * 512)
            for blk in range(4):
                load_norm_T(x_own, r0 + HALO + blk * 128, 128, blk * 128)
            load_norm_T(x_own, r0, HALO, 512)
            load_norm_T(x_own, r0 + HALO + 512, HALO, 528)

            for c in range(4):
                pq = proj_fm(QC + c * 128)
                norm_rope(pq, P_GQ, QT[:, c, :], QT)
            ck("qnorm")
            for c in range(4):
                pg = proj_fm(GA + c * 128)
                k.op("act", lambda e, c=c, pg=pg: e.activation(gatt[:, c, :], pg[:, :], AF.Silu), reads=[pg], writes=[gatt])
            ck("gates")
            for c in range(2):
                pa = proj_fm(CA + c * 128)
                pb = proj_fm(CB + c * 128)
                k.op("act", lambda e, pb=pb: e.activation(tB[:], pb[:, :], AF.Sigmoid), reads=[pb], writes=[tB])
                k.op("dve", lambda e, c=c, pa=pa: e.tensor_tensor(hpad[:, c, HALO:HALO + 512], pa[:, :], tB[:], op=ALU.mult),
                     reads=[pa, tB], writes=[hpad])
                pa2 = proj_fm(CA + c * 128, n0=512, n=32)
                pb2 = proj_fm(CB + c * 128, n0=512, n=32)
                k.op("act", lambda e, pb2=pb2: e.activation(tC[:, 0:32], pb2[:, 0:32], AF.Sigmoid), reads=[pb2], writes=[tC])
                k.op("dve", lambda e, c=c, pa2=pa2: e.tensor_tensor(hpad[:, c, 0:HALO], pa2[:, 0:HALO], tC[:, 0:HALO], op=ALU.mult),
                     reads=[pa2, tC], writes=[hpad])
                k.op("dve", lambda e, c=c, pa2=pa2: e.tensor_tensor(hpad[:, c, HALO + 512:544], pa2[:, HALO:32], tC[:, HALO:32], op=ALU.mult),
                     reads=[pa2, tC], writes=[hpad])
            for c in range(2):
                pg = proj_fm(GC + c * 128)
                k.op("act", lambda e, c=c, pg=pg: e.activation(gconv[:, c, :], pg[:, :], AF.Silu), reads=[pg], writes=[gconv])
            ck("glu")
            for c in range(2):
                eng = "dve"
                k.op(eng, lambda e, c=c: e.tensor_scalar(acc[:, c, :], hpad[:, c, 1:513], pcol(P_DW + c * 31), pcol(P_CB + c),
                                                         op0=ALU.mult, op1=ALU.add), reads=[hpad, prm], writes=[acc])
                for j in range(1, 31):
                    k.op(eng, lambda e, c=c, j=j: e.scalar_tensor_tensor(acc[:, c, :], hpad[:, c, 1 + j:513 + j], pcol(P_DW + c * 31 + j),
                                                                         acc[:, c, :], op0=ALU.mult, op1=ALU.add),
                         reads=[hpad, prm, acc], writes=[acc])
            ck("conv")
            for c in range(2):
                k.op("act", lambda e, c=c: e.activation(cb16[:, c, :], acc[:, c, :], AF.Copy), reads=[acc], writes=[cb16])
                k.op("act", lambda e, c=c: e.activation(csq16[:, c, :], acc[:, c, :], AF.Square), reads=[acc], writes=[csq16])
            pM1 = bank()
            pM2 = bank()
            for c in range(2):
                k.op("pe", lambda e, c=c: e.matmul(pM1[:, :], lhsT=o256, rhs=cb16[:, c, :], start=(c == 0), stop=(c == 1)),
                     reads=[cbf, cb16], writes=[pM1])
            for c in range(2):
                k.op("pe", lambda e, c=c: e.matmul(pM2[:, :], lhsT=o256, rhs=csq16[:, c, :], start=(c == 0), stop=(c == 1)),
                     reads=[cbf, csq16], writes=[pM2])
            k.op("act", lambda e: e.activation(tA[:], pM1[:, :], AF.Square), reads=[pM1], writes=[tA])
            k.op("dve", lambda e: e.tensor_tensor(tA[:], pM2[:, :], tA[:], op=ALU.subtract), reads=[pM2, tA], writes=[tA])
            k.op("act", lambda e: e.activation(tB[:], tA[:], AF.Sqrt, bias=pcol(P_EPS)), reads=[tA, prm], writes=[tB])
            k.op("dve", lambda e: e.reciprocal(tB[:], tB[:]), reads=[tB], writes=[tB])
            for c in range(2):
                k.op("dve", lambda e, c=c: e.tensor_tensor(tC[:], acc[:, c, :], pM1[:, :], op=ALU.subtract), reads=[acc, pM1], writes=[tC])
                k.op("dve", lambda e: e.tensor_tensor(tC[:], tC[:], tB[:], op=ALU.mult), reads=[tC, tB], writes=[tC])
                k.op("act", lambda e, c=c: e.activation(tA[:], tC[:], AF.Silu, scale=pcol(P_CLG + c), bias=pcol(P_CLB + c)),
                     reads=[tC, prm], writes=[tA])
                k.op("dve", lambda e, c=c: e.tensor_tensor(mixT[:, 4 + c, :], tA[:], gconv[:, c, :], op=ALU.mult),
                     reads=[tA, gconv], writes=[mixT])
            ck("convln")
            for blk in range(4):
                pu = [bank(), bank()]
                for kc in range(8):
                    k.op("pe", lambda e, kc=kc, blk=blk: e.matmul(pu[0][:, :], lhsT=hT[:, kc, blk * 128:(blk + 1) * 128],
                                                                  rhs=wbf[:, kc, SG:SG + 512], start=(kc == 0), stop=(kc == 7)),
                         reads=[wbf, hT], writes=[pu[0]])
                for kc in range(8):
                    k.op("pe", lambda e, kc=kc, blk=blk: e.matmul(pu[1][:, 0:256], lhsT=hT[:, kc, blk * 128:(blk + 1) * 128],
                                                                  rhs=wbf[:, kc, SG + 512:SG + 768], start=(kc == 0), stop=(kc == 7)),
                         reads=[wbf, hT], writes=[pu[1]])
                k.op("act", lambda e: e.activation(gu[:], pu[0][:, 0:256], AF.Gelu), reads=[pu[0]], writes=[gu])
                k.op("act", lambda e: e.activation(gv[:], pu[0][:, 256:512], AF.Gelu, accum_out=st[:, 0:1]), reads=[pu[0]], writes=[gv, st])
                k.op("act", lambda e: e.activation(gg[:], pu[1][:, 0:256], AF.Silu), reads=[pu[1]], writes=[gg])
                k.op("act", lambda e: e.activation(tC[:, 0:256], gv[:], AF.Square, accum_out=st[:, 1:2]), reads=[gv], writes=[tC, st])
                k.op("dve", lambda e: e.tensor_scalar(st[:, 2:4], st[:, 0:2], 1.0 / 256, None, op0=ALU.mult), reads=[st], writes=[st])
                k.op("dve", lambda e: e.tensor_tensor(st[:, 4:5], st[:, 2:3], st[:, 2:3], op=ALU.mult), reads=[st], writes=[st])
                k.op("dve", lambda e: e.tensor_tensor(st[:, 5:6], st[:, 3:4], st[:, 4:5], op=ALU.subtract), reads=[st], writes=[st])
                k.op("act", lambda e: e.activation(st[:, 6:7], st[:, 5:6], AF.Sqrt, bias=pcol(P_EPS)), reads=[st, prm], writes=[st])
                k.op("dve", lambda e: e.reciprocal(st[:, 7:8], st[:, 6:7]), reads=[st], writes=[st])
                k.op("dve", lambda e: e.tensor_scalar(gv[:], gv[:], st[:, 2:3], st[:, 7:8], op0=ALU.subtract, op1=ALU.mult),
                     reads=[gv, st], writes=[gv])
                k.op("dve", lambda e: e.tensor_tensor(gv[:], gv[:], bc[:, B_SLG:B_SLG + 256], op=ALU.mult), reads=[gv, bc], writes=[gv])
                k.op("dve", lambda e: e.tensor_tensor(vln[:], gv[:], bc[:, B_SLB:B_SLB + 256], op=ALU.add), reads=[gv, bc], writes=[vln])
                pm = bank()
                for h in range(4):
                    k.op("pe", lambda e, h=h: e.matmul(pm[:, h * 64:(h + 1) * 64], lhsT=wsbf[:, h, :], rhs=vln[:, h * 64:(h + 1) * 64],
                                                       start=True, stop=True), reads=[wsbf, vln], writes=[pm])
                k.op("dve", lambda e: e.tensor_tensor(gv[:].rearrange("p (h d) -> p h d", h=4), pm[:, 0:256].rearrange("p (h d) -> p h d", h=4),
                                                      bc[:, B_SGB:B_SGB + 4].unsqueeze(2).to_broadcast([128, 4, 64]), op=ALU.add),
                     reads=[pm, bc], writes=[gv])
                k.op("dve", lambda e: e.tensor_tensor(gv[:], gv[:], gu[:], op=ALU.mult), reads=[gv, gu], writes=[gv])
                k.op("dve", lambda e, blk=blk: e.tensor_tensor(sgt[:, blk, :], gv[:], gg[:], op=ALU.mult), reads=[gv, gg], writes=[sgt])
            for c in range(2):
                pt = pT[c]
                for blk in range(4):
                    k.op("pe", lambda e, c=c, blk=blk: e.transpose(pt[:, blk * 128:(blk + 1) * 128], sgt[:, blk, c * 128:(c + 1) * 128], ident),
                         reads=[sgt, cbf], writes=[pt])
                k.op("act", lambda e, c=c: e.activation(mixT[:, 6 + c, :], pt[:, 0:512], AF.Copy), reads=[pt], writes=[mixT])

            ck("sgu")
            Sb = [S0, S1]
            St = [S0t, S1t]
            for c in range(4):
                def qk(kt):
                    s = Sb[kt % 2]
                    k.op("pe", lambda e: e.matmul(s[0][:, :], lhsT=KT[0:64, kt * 128:(kt + 1) * 128], rhs=QT[0:64, c, :], start=True, stop=True),
                         reads=[KT, QT], writes=[s[0]])
                    k.op("pe", lambda e: e.matmul(s[1][:, :], lhsT=KT[64:128, kt * 128:(kt + 1) * 128], rhs=QT[64:128, c, :], start=True, stop=True),
                         reads=[KT, QT], writes=[s[1]])
                qk(0)
                for kt in range(NK):
                    if kt + 1 < NK:
                        qk(kt + 1)
                    s = Sb[kt % 2]
                    p = Pb[kt % 2]
                    k.op("act", lambda e: e.activation(p[:], St[kt % 2][:, :], AF.Exp, scale=0.125), reads=s, writes=[p])
                    k.op("pe", lambda e: e.matmul(OE[0:65, :], lhsT=VA[:, kt, 0, 0:65], rhs=p[:, 0:512], start=(kt == 0), stop=(kt == NK - 1)),
                         reads=[VA, p], writes=[OE])
                    k.op("pe", lambda e: e.matmul(OO[0:65, :], lhsT=VA[:, kt, 1, 0:65], rhs=p[:, 512:1024], start=(kt == 0), stop=(kt == NK - 1)),
                         reads=[VA, p], writes=[OO])
                k.op("act", lambda e: e.activation(Osb[0:65, 0:512], OE[0:65, :], AF.Copy), reads=[OE], writes=[Osb])
                k.op("dve", lambda e: e.tensor_copy(Osb[0:65, 512:1024], OO[0:65, :]), reads=[OO], writes=[Osb])
                k.op("dve", lambda e: e.reciprocal(Osb[64:65, :], Osb[64:65, :]), reads=[Osb], writes=[Osb])
                k.op("pe", lambda e: e.matmul(pBc[0][0:64, :], lhsT=onesf[64:65, 0:64], rhs=Osb[64:65, 0:512], start=True, stop=True),
                     reads=[onesf, Osb], writes=[pBc[0]])
                k.op("pe", lambda e: e.matmul(pBc[1][0:64, :], lhsT=onesf[64:65, 0:64], rhs=Osb[64:65, 512:1024], start=True, stop=True),
                     reads=[onesf, Osb], writes=[pBc[1]])
                k.op("dve", lambda e: e.tensor_tensor(tmpA[0:64, :], Osb[0:64, 0:512], pBc[0][0:64, :], op=ALU.mult), reads=[Osb, pBc[0]], writes=[tmpA])
                k.op("dve", lambda e: e.tensor_tensor(tmpB[0:64, :], Osb[0:64, 512:1024], pBc[1][0:64, :], op=ALU.mult), reads=[Osb, pBc[1]], writes=[tmpB])
                k.dma("sp", "fin", [(tmpA[64:128, :], tmpB[0:64, :])], reads=[tmpB], writes=[tmpA])
                k.op("dve", lambda e, c=c: e.tensor_tensor(mixT[:, c, :], tmpA[:], gatt[:, c, :], op=ALU.mult), reads=[tmpA, gatt], writes=[mixT])

            if dbg and "mixT" in dbg and i == 0:
                k.dma("sp", "dbg", [(dbg_d["mixT"].rearrange("(c p) t -> p c t", p=128), mixT[:])], reads=[mixT])

            ck("att")
            for blk in range(4):
                sbuf = Sb[blk % 2]
                stt = St[blk % 2]
                for half in range(2):
                    for mc in range(8):
                        k.op("pe", lambda e, mc=mc, half=half: e.matmul(sbuf[half][:, :], lhsT=mixT[:, mc, blk * 128:(blk + 1) * 128],
                                                                        rhs=wobf[:, mc, half * 512:(half + 1) * 512], start=(mc == 0), stop=(mc == 7)),
                             reads=[mixT, wobf], writes=[sbuf[half]])
                xi = xslot[0]
                xslot[0] += 1
                xr = xs[xi % 2]
                k.dma("sp", f"xs{xi % 2}", [(xr[:], x_own[r0 + HALO + blk * 128:r0 + HALO + (blk + 1) * 128, :])], writes=[xr])
                k.op("act", lambda e: e.activation(yb[:], stt[:, :], AF.Square, accum_out=st[:, 8:9]), reads=sbuf, writes=[yb, st])
                k.op("dve", lambda e: e.tensor_scalar(st[:, 9:10], st[:, 8:9], 1.0 / D, EPS, op0=ALU.mult, op1=ALU.add), reads=[st], writes=[st])
                k.op("act", lambda e: e.activation(st[:, 10:11], st[:, 9:10], AF.Sqrt), reads=[st], writes=[st])
                k.op("dve", lambda e: e.reciprocal(st[:, 11:12], st[:, 10:11]), reads=[st], writes=[st])
                k.op("dve", lambda e: e.scalar_tensor_tensor(yb[:], stt[:, :], st[:, 11:12], bc[:, B_POST:B_POST + D], op0=ALU.mult, op1=ALU.mult),
                     reads=sbuf + [st, bc], writes=[yb])
                k.op("dve", lambda e: e.tensor_tensor(yb[:], yb[:], xr[:], op=ALU.add), reads=[yb, xr], writes=[yb])
                k.dma("sp", "yout", [(y_d[i * 512 + blk * 128:i * 512 + (blk + 1) * 128, :], yb[:])], reads=[yb])

    try:
        body()
    except _Stop:
        pass

    outb = Buf(None)
    for nm in ("yout", "dbg"):
        if nm in k.dsems:
            outb.d.lw = (k.dsems[nm], k.dcnt[nm])
            k.wait_all("sp", [outb])
    return nc


def _rope_tables(pos):
    pos = np.asarray(pos)
    inv = (10000.0 ** (-np.arange(16, dtype=np.float32) / 16)).astype(np.float32)
    row = (pos // 64).astype(np.float32)
    col = (pos % 64).astype(np.float32)
    d = np.arange(128) % 64
    p = np.where((d < 32)[:, None], row[None, :], col[None, :]).astype(np.float32)
    f = inv[(d % 32) % 16][:, None]
    ang = (p * f).astype(np.float32)
    return np.stack([np.cos(ang), np.sin(ang)]).astype(np.float32)


def _consts():
    c = np.zeros((128, 4, 128), np.float32)
    c[:, 0, :] = np.eye(128)
    for kk in range(128):
        if kk % 32 >= 16:
            c[kk, 1, kk - 16] = -1.0
        else:
            c[kk, 1, kk + 16] = 1.0
    for kk in range(128):
        c[kk, 2, (kk // 64) * 64:(kk // 64) * 64 + 64] = 1.0 / 64
    c[:, 3, :] = 1.0 / 256
    return c.astype(ml_dtypes.bfloat16)


def _layer_params(l, pre_norm, post_norm, w_in, w_out, q_norm, k_norm, conv_dw, conv_dw_b,
                  conv_ln_g, conv_ln_b, sg_ln_g, sg_ln_b, sg_w, sg_b):
    perm = np.arange(512).reshape(8, 64)
    hp = np.concatenate([np.concatenate([perm[c], perm[4 + c]]) for c in range(4)])
    cols = np.concatenate([hp, 512 + np.arange(256), 768 + hp, np.arange(1280, DIN)])
    wi = np.ascontiguousarray(w_in[l][:, cols])
    rows = np.concatenate([hp, np.arange(512, 1024)])
    wo = np.ascontiguousarray(w_out[l][rows, :])
    prm = np.zeros((128, P_N), np.float32)
    prm[:, P_GPRE:P_GPRE + 8] = pre_norm[l].reshape(8, 128).T
    prm[:, P_GQ] = np.tile(q_norm[l], 2)
    prm[:, P_GK] = np.tile(k_norm[l], 2)
    prm[:, P_EPS] = EPS
    prm[:, P_CB:P_CB + 2] = conv_dw_b[l].reshape(2, 128).T
    prm[:, P_CLG:P_CLG + 2] = conv_ln_g[l].reshape(2, 128).T
    prm[:, P_CLB:P_CLB + 2] = conv_ln_b[l].reshape(2, 128).T
    prm[:, P_DW:P_DW + 62] = conv_dw[l].T.reshape(2, 128, 31).transpose(1, 0, 2).reshape(128, 62)
    bcv = np.zeros((128, B_N), np.float32)
    bcv[:, B_POST:B_POST + D] = post_norm[l][None, :]
    bcv[:, B_SLG:B_SLG + 256] = sg_ln_g[l][None, :]
    bcv[:, B_SLB:B_SLB + 256] = sg_ln_b[l][None, :]
    bcv[:, B_SGB:B_SGB + 4] = sg_b[l].T
    wst = np.ascontiguousarray(sg_w[l].transpose(2, 0, 1))
    return {"w_in": wi, "w_out": wo, "prm": prm, "bc": bcv, "wst": wst}


_NC_CACHE = {}


def run_layer(x, l, params, dbg=None):
    B, S, _ = x.shape
    per = NCORES // B
    TOWN = S // per
    key = (TOWN, S, tuple(sorted(dbg.items())) if dbg else None)
    nc = build_layer(TOWN, S, dbg)
    lp = _layer_params(l, **params)
    cbf = _consts()
    ropek = _rope_tables(np.arange(S))
    in_maps = []
    for c in range(NCORES):
        b, r = divmod(c, per)
        t0 = r * TOWN
        xo = np.zeros((TOWN + 2 * HALO, D), np.float32)
        lo, hi = max(0, t0 - HALO), min(S, t0 + TOWN + HALO)
        xo[lo - (t0 - HALO):hi - (t0 - HALO)] = x[b, lo:hi]
        m = {"x_all": np.ascontiguousarray(x[b]), "x_own": xo, "ropek": ropek,
             "ropeq": _rope_tables(np.arange(t0, t0 + TOWN)), "cbf": cbf}
        m.update(lp)
        in_maps.append(m)
    res = run_bass_kernel_spmd(nc, in_maps, core_ids=list(range(NCORES)))
    y = np.empty_like(x)
    for c in range(NCORES):
        b, r = divmod(c, per)
        y[b, r * TOWN:(r + 1) * TOWN] = res.results[c]["y"]
    return y, res


def kernel(x, pre_norm, post_norm, w_in, w_out, q_norm, k_norm, conv_dw, conv_dw_b,
           conv_ln_g, conv_ln_b, sg_ln_g, sg_ln_b, sg_w, sg_b):
    params = dict(pre_norm=pre_norm, post_norm=post_norm, w_in=w_in, w_out=w_out, q_norm=q_norm, k_norm=k_norm,
                  conv_dw=conv_dw, conv_dw_b=conv_dw_b, conv_ln_g=conv_ln_g, conv_ln_b=conv_ln_b,
                  sg_ln_g=sg_ln_g, sg_ln_b=sg_ln_b, sg_w=sg_w, sg_b=sg_b)
    params = {kk: np.asarray(v, np.float32) for kk, v in params.items()}
    x = np.asarray(x, np.float32)
    for l in range(pre_norm.shape[0]):
        x, _ = run_layer(x, l, params)
    return x
```

```python
import numpy as np
import ml_dtypes
import concourse.bass as bass
import concourse.mybir as mybir
from concourse.bass_utils import run_bass_kernel_spmd

F32 = mybir.dt.float32
BF16 = mybir.dt.bfloat16
AF = mybir.ActivationFunctionType
ALU = mybir.AluOpType

D = 1024
DIN = 2816
NCORES = 8
EPS = 1e-6
HALO = 16

QC, KC, VC, GA, CA, CB, GC, SG = 0, 512, 640, 768, 1280, 1536, 1792, 2048
P_GPRE, P_GQ, P_GK, P_EPS, P_CB, P_CLG, P_CLB, P_DW, P_N = 0, 8, 9, 10, 11, 13, 15, 17, 17 + 62
B_POST, B_SLG, B_SLB, B_SGB, B_N = 0, 1024, 1280, 1536, 1540


class Dep:
    __slots__ = ("lw", "rd")

    def __init__(self):
        self.lw = None
        self.rd = {}


class Buf:
    __slots__ = ("t", "d", "name", "psum")

    def __init__(self, t, name="", d=None, psum=False):
        self.t = t
        self.d = d if d is not None else Dep()
        self.name = name
        self.psum = psum

    def __getitem__(self, idx):
        return self.t[idx]

    def alias(self, ap, name=""):
        return Buf(ap, name, self.d, self.psum)


class K:
    def __init__(self, nc, same_engine_sync=True):
        self.nc = nc
        self.eng = {"pe": nc.tensor, "act": nc.scalar, "dve": nc.vector, "pool": nc.gpsimd, "sp": nc.sync}
        self.sem = {e: nc.alloc_semaphore("prog_" + e) for e in self.eng}
        self.cnt = {e: 0 for e in self.eng}
        self.seen = {e: {} for e in self.eng}
        self.same = same_engine_sync
        self.nbuf = 0
        self.dsems = {}
        self.dcnt = {}

    def sb(self, shape, dt, name=None):
        self.nbuf += 1
        name = "s_" + (name or f"sb{self.nbuf}")
        return Buf(self.nc.alloc_sbuf_tensor(name, list(shape), dt), name)

    def _deps(self, e, reads, writes, attach_ok=False):
        need = {}
        own = self.sem[e]
        for b in reads:
            if b.d.lw is not None:
                s, v = b.d.lw
                if need.get(s, 0) < v:
                    need[s] = v
            if b.psum:
                for s, v in b.d.rd.items():
                    if s is not own and need.get(s, 0) < v:
                        need[s] = v
        for b in writes:
            if b.d.lw is not None:
                s, v = b.d.lw
                if need.get(s, 0) < v:
                    need[s] = v
            for s, v in b.d.rd.items():
                if need.get(s, 0) < v:
                    need[s] = v
        own = self.sem[e]
        seen = self.seen[e]
        todo = []
        for s, v in need.items():
            if s is own and (e == "pe" or not self.same):
                continue
            if seen.get(s, 0) >= v:
                continue
            todo.append((s, v))
            seen[s] = v
        attach = todo.pop() if (todo and attach_ok) else None
        for s, v in todo:
            self.eng[e].wait_ge(s, v)
        return attach

    def _done(self, tok, reads, writes):
        s, v = tok
        for b in writes:
            b.d.lw = tok
            b.d.rd = {}
        for b in reads:
            if b.d.rd.get(s, 0) < v:
                b.d.rd[s] = v

    def op(self, e, fn, reads=(), writes=()):
        att = self._deps(e, reads, writes, attach_ok=True)
        ins = fn(self.eng[e])
        if att is not None:
            ins._wait_ge(att[0], att[1])
        self.cnt[e] += 1
        ins.then_inc(self.sem[e], 1)
        self._done((self.sem[e], self.cnt[e]), reads, writes)
        return ins

    def dma(self, q, semname, pairs, reads=(), writes=()):
        if semname not in self.dsems:
            self.dsems[semname] = self.nc.alloc_semaphore("dma_" + semname)
            self.dcnt[semname] = 0
        s = self.dsems[semname]
        self._deps(q, reads, writes)
        for o, i in pairs:
            self.eng[q].dma_start(out=o, in_=i).then_inc(s, 16)
            self.dcnt[semname] += 16
        self._done((s, self.dcnt[semname]), reads, writes)

    def wait_all(self, e, bufs):
        self._deps(e, bufs, [])


class _Stop(Exception):
    pass


def build_layer(TOWN, SKV, dbg=None, stop=None):
    NT = TOWN // 512
    NKT = SKV // 512
    NK = SKV // 128
    nc = bass.Bass("TRN2", target_bir_lowering=False)
    k = K(nc)

    def dram(name, shape, dt=F32, kind="ExternalInput"):
        return nc.dram_tensor(name, list(shape), dt, kind=kind).ap()

    x_all = dram("x_all", [SKV, D])
    x_own = dram("x_own", [TOWN + 2 * HALO, D])
    w_in = dram("w_in", [D, DIN])
    w_out = dram("w_out", [D, D])
    ropek = dram("ropek", [2, 128, SKV])
    ropeq = dram("ropeq", [2, 128, TOWN])
    cbf_d = dram("cbf", [128, 4, 128], BF16)
    prm_d = dram("prm", [128, P_N])
    bc_d = dram("bc", [128, B_N])
    wst_d = dram("wst", [128, 4, 128])
    y_d = dram("y", [TOWN, D], kind="ExternalOutput")
    dbg_d = {}
    if dbg:
        for nm, shp in dbg.items():
            dbg_d[nm] = dram("dbg_" + nm, shp, kind="ExternalOutput")

    KT = k.sb([128, SKV], BF16, "KT")
    VA = k.sb([128, NK, 2, 66], BF16, "VA")
    wbf = k.sb([128, 8, DIN], BF16, "wbf")
    wobf = k.sb([128, 8, D], BF16, "wobf")
    wsbf = k.sb([128, 4, 128], BF16, "wsbf")
    cbf = k.sb([128, 4, 128], BF16, "cbf")
    prm = k.sb([128, P_N], F32, "prm")
    bc = k.sb([128, B_N], F32, "bc")
    xs = [k.sb([128, D], F32, f"xs{i}") for i in range(2)]
    xbs = [k.sb([128, D], BF16, f"xb{i}") for i in range(2)]
    yb = k.sb([128, D], F32, "yb")
    hT = k.sb([128, 8, 544], BF16, "hT")
    cosb = k.sb([128, 512], F32, "cosb")
    sinb = k.sb([128, 512], F32, "sinb")
    tA = k.sb([128, 512], F32, "tA")
    tB = k.sb([128, 512], F32, "tB")
    tC = k.sb([128, 512], F32, "tC")
    sqb = k.sb([128, 512], BF16, "sqb")
    qgb = k.sb([128, 512], BF16, "qgb")
    QT = k.sb([128, 4, 512], BF16, "QT")
    gatt = k.sb([128, 4, 512], BF16, "gatt")
    hpad = k.sb([128, 2, 544], F32, "hpad")
    acc = k.sb([128, 2, 512], F32, "acc")
    gconv = k.sb([128, 2, 512], BF16, "gconv")
    vln = k.sb([128, 256], BF16, "vln")
    mixT = k.sb([128, 8, 512], BF16, "mixT")
    Pb = [k.sb([128, 1024], BF16, f"P{i}") for i in range(2)]
    st = k.sb([128, 32], F32, "st")
    cb16 = Pb[0].alias(Pb[0][:, :].rearrange("p (a b) -> p a b", a=2), "cb16")
    csq16 = Pb[1].alias(Pb[1][:, :].rearrange("p (a b) -> p a b", a=2), "csq16")
    sgt = gconv.alias(gconv[:, :, :].rearrange("p a (c d) -> p (a c) d", c=2), "sgt")
    Osb = acc.alias(acc[:, :, :].rearrange("p a b -> p (a b)"), "Osb")
    gu = tA.alias(tA[:, 0:256], "gu")
    gv = tA.alias(tA[:, 256:512], "gv")
    gg = tB.alias(tB[:, 0:256], "gg")
    tmpA = qgb
    tmpB = sqb
    onesf = k.sb([128, 64], F32, "onesf")

    S0t = nc.alloc_psum_tensor("S0", [128, 1024], F32)
    S1t = nc.alloc_psum_tensor("S1", [128, 1024], F32)
    OEt = nc.alloc_psum_tensor("OE", [128, 512], F32)
    OOt = nc.alloc_psum_tensor("OO", [128, 512], F32)
    pT = [Buf(nc.alloc_psum_tensor(f"pT{i}", [128, 1024], BF16), f"pT{i}", psum=True) for i in range(2)]
    S0 = [Buf(S0t[:, 0:512], "S0a", psum=True), Buf(S0t[:, 512:1024], "S0b", psum=True)]
    S1 = [Buf(S1t[:, 0:512], "S1a", psum=True), Buf(S1t[:, 512:1024], "S1b", psum=True)]
    OE = Buf(OEt[:, :], "OE", psum=True)
    OO = Buf(OOt[:, :], "OO", psum=True)
    pBc = [pT[i].alias(pT[i][:, :].bitcast(F32), f"pBc{i}") for i in range(2)]
    banks = [OE, OO, S0[0], S0[1], S1[0], S1[1]]
    bank_i = [0]

    def bank():
        b = banks[bank_i[0] % len(banks)]
        bank_i[0] += 1
        return b

    ident = cbf[:, 0, :]
    RT = cbf[:, 1, :]
    bones = cbf[:, 2, :]
    o256 = cbf[:, 3, :]

    def pcol(c, rows=128):
        return prm[0:rows, c:c + 1]

    def ck(name):
        if stop == name:
            raise _Stop()

    def body():
        k.dma("sp", "c0", [(cbf[:], cbf_d[:, :, :]), (prm[:], prm_d[:, :]), (bc[:], bc_d[:, :])], writes=[cbf, prm, bc])
        k.op("pool", lambda e: e.memset(VA[:], 1.0), writes=[VA])
        k.op("dve", lambda e: e.memset(st[:], 0.0), writes=[st])
        k.op("dve", lambda e: e.memset(onesf[:], 1.0), writes=[onesf])
        k.dma("sp", "xs0", [(xs[0][:, 0:512], wst_d.rearrange("q h p -> q (h p)"))], writes=[xs[0]])
        k.op("dve", lambda e: e.tensor_copy(wsbf[:].rearrange("q h p -> q (h p)"), xs[0][:, 0:512]), reads=[xs[0]], writes=[wsbf])
        si = 1
        for kc in range(8):
            for (c0, c1) in ((0, 1024), (1024, 2048), (2048, DIN)):
                s = xs[si % 2]
                k.dma("sp", f"xs{si % 2}", [(s[:, 0:c1 - c0], w_in[kc * 128:(kc + 1) * 128, c0:c1])], writes=[s])
                if si % 2:
                    k.op("dve", lambda en, s=s, kc=kc, c0=c0, c1=c1: en.tensor_scalar(
                        wbf[:, kc, c0:c1], s[:, 0:c1 - c0], pcol(P_GPRE + kc), None, op0=ALU.mult), reads=[s, prm], writes=[wbf])
                else:
                    k.op("act", lambda en, s=s, kc=kc, c0=c0, c1=c1: en.activation(
                        wbf[:, kc, c0:c1], s[:, 0:c1 - c0], AF.Copy, scale=pcol(P_GPRE + kc)), reads=[s, prm], writes=[wbf])
                si += 1
        for kc in range(8):
            s = xs[si % 2]
            k.dma("sp", f"xs{si % 2}", [(s[:], w_out[kc * 128:(kc + 1) * 128, :])], writes=[s])
            if si % 2:
                k.op("dve", lambda en, s=s, kc=kc: en.tensor_copy(wobf[:, kc, :], s[:]), reads=[s], writes=[wobf])
            else:
                k.op("act", lambda en, s=s, kc=kc: en.activation(wobf[:, kc, :], s[:], AF.Copy), reads=[s], writes=[wobf])
            si += 1
        xslot = [si]

        def load_norm_T(src, row0, nrows, col0):
            i = xslot[0]
            xslot[0] += 1
            s, xb, pt = xs[i % 2], xbs[i % 2], pT[i % 2]
            c = i % 8
            k.dma("sp", f"xs{i % 2}", [(s[0:nrows, :], src[row0:row0 + nrows, :])], writes=[s])
            k.op("act", lambda e: e.activation(yb[0:nrows, :], s[0:nrows, :], AF.Square, accum_out=st[0:nrows, c:c + 1]),
                 reads=[s], writes=[yb, st])
            k.op("dve", lambda e: e.tensor_scalar(st[0:nrows, 8 + c:9 + c], st[0:nrows, c:c + 1], 1.0 / D, EPS,
                                                  op0=ALU.mult, op1=ALU.add), reads=[st], writes=[st])
            k.op("act", lambda e: e.activation(st[0:nrows, 16 + c:17 + c], st[0:nrows, 8 + c:9 + c], AF.Sqrt), reads=[st], writes=[st])
            k.op("dve", lambda e: e.reciprocal(st[0:nrows, 24 + c:25 + c], st[0:nrows, 16 + c:17 + c]), reads=[st], writes=[st])
            k.op("dve", lambda e: e.tensor_scalar(xb[0:nrows, :], s[0:nrows, :], st[0:nrows, 24 + c:25 + c], None, op0=ALU.mult),
                 reads=[s, st], writes=[xb])
            for kc in range(8):
                k.op("pe", lambda e, kc=kc: e.transpose(pt[:, kc * 128:kc * 128 + nrows], xb[0:nrows, kc * 128:(kc + 1) * 128],
                                                        ident[0:nrows, 0:nrows]), reads=[xb, cbf], writes=[pt])
            src_ap = pt[:, :].rearrange("p (a b) -> p a b", a=8)[:, :, 0:nrows]
            k.op("act", lambda e: e.activation(hT[:, :, col0:col0 + nrows], src_ap, AF.Copy), reads=[pt], writes=[hT])

        def proj_fm(col0, n0=0, n=512, m=128):
            b = bank()
            for kc in range(8):
                k.op("pe", lambda e, kc=kc: e.matmul(b[0:m, 0:n], lhsT=wbf[:, kc, col0:col0 + m], rhs=hT[:, kc, n0:n0 + n],
                                                     start=(kc == 0), stop=(kc == 7)), reads=[wbf, hT], writes=[b])
            return b

        def norm_rope(pP, gcol, out_ap, out_buf):
            k.op("act", lambda e: e.activation(sqb[:], pP[:, :], AF.Square), reads=[pP], writes=[sqb])
            k.op("dve", lambda e: e.tensor_scalar(qgb[:], pP[:, :], pcol(gcol), None, op0=ALU.mult), reads=[pP, prm], writes=[qgb])
            k.op("dve", lambda e: e.scalar_tensor_tensor(tA[:], pP[:, :], pcol(gcol), cosb[:], op0=ALU.mult, op1=ALU.mult),
                 reads=[pP, prm, cosb], writes=[tA])
            pMS = bank()
            k.op("pe", lambda e: e.matmul(pMS[:, :], lhsT=bones, rhs=sqb[:], start=True, stop=True), reads=[cbf, sqb], writes=[pMS])
            pRO = bank()
            k.op("pe", lambda e: e.matmul(pRO[:, :], lhsT=RT, rhs=qgb[:], start=True, stop=True), reads=[cbf, qgb], writes=[pRO])
            k.op("act", lambda e: e.activation(tB[:], pMS[:, :], AF.Sqrt, bias=pcol(P_EPS)), reads=[pMS, prm], writes=[tB])
            k.op("dve", lambda e: e.reciprocal(tB[:], tB[:]), reads=[tB], writes=[tB])
            k.op("dve", lambda e: e.tensor_tensor(tC[:], pRO[:, :], sinb[:], op=ALU.mult), reads=[pRO, sinb], writes=[tC])
            k.op("dve", lambda e: e.tensor_tensor(tA[:], tA[:], tC[:], op=ALU.add), reads=[tA, tC], writes=[tA])
            k.op("dve", lambda e: e.tensor_tensor(out_ap, tA[:], tB[:], op=ALU.mult), reads=[tA, tB], writes=[out_buf])

        def load_rope(tab, t0):
            k.dma("sp", "rope", [(cosb[:], tab[0, :, t0:t0 + 512]), (sinb[:], tab[1, :, t0:t0 + 512])], writes=[cosb, sinb])

        ck("setup")
        for j in range(NKT):
            load_rope(ropek, j * 512)
            ck("k_rope")
            for blk in range(4):
                load_norm_T(x_all, j * 512 + blk * 128, 128, blk * 128)
                ck("k_lnt1")
            pK = proj_fm(KC)
            ck("k_proj")
            norm_rope(pK, P_GK, KT[:, j * 512:(j + 1) * 512], KT)
            ck("k_nr")
            pV = bank()
            for blk in range(4):
                for kc in range(8):
                    k.op("pe", lambda e, kc=kc, blk=blk: e.matmul(pV[:, blk * 128:(blk + 1) * 128], lhsT=hT[:, kc, blk * 128:(blk + 1) * 128],
                                                                  rhs=wbf[:, kc, VC:VC + 128], start=(kc == 0), stop=(kc == 7)),
                         reads=[wbf, hT], writes=[pV])
            ck("k_vmm")
            k.op("act", lambda e: e.activation(VA[:, j * 4:(j + 1) * 4, :, 0:64],
                                               pV[:, :].rearrange("p (a g d) -> p a g d", a=4, g=2), AF.Copy), reads=[pV], writes=[VA])

        if dbg and "KT" in dbg:
            k.op("dve", lambda e: e.tensor_copy(tA[:], KT[:, 0:512]), reads=[KT], writes=[tA])
            k.dma("sp", "dbg", [(dbg_d["KT"][:, :], tA[:])], reads=[tA])

        ck("phaseK")
        for i in range(NT):
            r0 = i * 512
            load_rope(ropeq, i * 512)
            for blk in range(4):
                load_norm_T(x_own, r0 + HALO + blk * 128, 128, blk * 128)
            load_norm_T(x_own, r0, HALO, 512)
            load_norm_T(x_own, r0 + HALO + 512, HALO, 528)

            for c in range(4):
                pq = proj_fm(QC + c * 128)
                norm_rope(pq, P_GQ, QT[:, c, :], QT)
            ck("qnorm")
            for c in range(4):
                pg = proj_fm(GA + c * 128)
                k.op("act", lambda e, c=c, pg=pg: e.activation(gatt[:, c, :], pg[:, :], AF.Silu), reads=[pg], writes=[gatt])
            ck("gates")
            for c in range(2):
                pa = proj_fm(CA + c * 128)
                pb = proj_fm(CB + c * 128)
                k.op("act", lambda e, pb=pb: e.activation(tB[:], pb[:, :], AF.Sigmoid), reads=[pb], writes=[tB])
                k.op("dve", lambda e, c=c, pa=pa: e.tensor_tensor(hpad[:, c, HALO:HALO + 512], pa[:, :], tB[:], op=ALU.mult),
                     reads=[pa, tB], writes=[hpad])
                pa2 = proj_fm(CA + c * 128, n0=512, n=32)
                pb2 = proj_fm(CB + c * 128, n0=512, n=32)
                k.op("act", lambda e, pb2=pb2: e.activation(tC[:, 0:32], pb2[:, 0:32], AF.Sigmoid), reads=[pb2], writes=[tC])
                k.op("dve", lambda e, c=c, pa2=pa2: e.tensor_tensor(hpad[:, c, 0:HALO], pa2[:, 0:HALO], tC[:, 0:HALO], op=ALU.mult),
                     reads=[pa2, tC], writes=[hpad])
                k.op("dve", lambda e, c=c, pa2=pa2: e.tensor_tensor(hpad[:, c, HALO + 512:544], pa2[:, HALO:32], tC[:, HALO:32], op=ALU.mult),
                     reads=[pa2, tC], writes=[hpad])
            for c in range(2):
                pg = proj_fm(GC + c * 128)
                k.op("act", lambda e, c=c, pg=pg: e.activation(gconv[:, c, :], pg[:, :], AF.Silu), reads=[pg], writes=[gconv])
            ck("glu")
            for c in range(2):
                eng = "dve"
                k.op(eng, lambda e, c=c: e.tensor_scalar(acc[:, c, :], hpad[:, c, 1:513], pcol(P_DW + c * 31), pcol(P_CB + c),
                                                         op0=ALU.mult, op1=ALU.add), reads=[hpad, prm], writes=[acc])
                for j in range(1, 31):
                    k.op(eng, lambda e, c=c, j=j: e.scalar_tensor_tensor(acc[:, c, :], hpad[:, c, 1 + j:513 + j], pcol(P_DW + c * 31 + j),
                                                                         acc[:, c, :], op0=ALU.mult, op1=ALU.add),
                         reads=[hpad, prm, acc], writes=[acc])
            ck("conv")
            for c in range(2):
                k.op("act", lambda e, c=c: e.activation(cb16[:, c, :], acc[:, c, :], AF.Copy), reads=[acc], writes=[cb16])
                k.op("act", lambda e, c=c: e.activation(csq16[:, c, :], acc[:, c, :], AF.Square), reads=[acc], writes=[csq16])
            pM1 = bank()
            pM2 = bank()
            for c in range(2):
                k.op("pe", lambda e, c=c: e.matmul(pM1[:, :], lhsT=o256, rhs=cb16[:, c, :], start=(c == 0), stop=(c == 1)),
                     reads=[cbf, cb16], writes=[pM1])
            for c in range(2):
                k.op("pe", lambda e, c=c: e.matmul(pM2[:, :], lhsT=o256, rhs=csq16[:, c, :], start=(c == 0), stop=(c == 1)),
                     reads=[cbf, csq16], writes=[pM2])
            k.op("act", lambda e: e.activation(tA[:], pM1[:, :], AF.Square), reads=[pM1], writes=[tA])
            k.op("dve", lambda e: e.tensor_tensor(tA[:], pM2[:, :], tA[:], op=ALU.subtract), reads=[pM2, tA], writes=[tA])
            k.op("act", lambda e: e.activation(tB[:], tA[:], AF.Sqrt, bias=pcol(P_EPS)), reads=[tA, prm], writes=[tB])
            k.op("dve", lambda e: e.reciprocal(tB[:], tB[:]), reads=[tB], writes=[tB])
            for c in range(2):
                k.op("dve", lambda e, c=c: e.tensor_tensor(tC[:], acc[:, c, :], pM1[:, :], op=ALU.subtract), reads=[acc, pM1], writes=[tC])
                k.op("dve", lambda e: e.tensor_tensor(tC[:], tC[:], tB[:], op=ALU.mult), reads=[tC, tB], writes=[tC])
                k.op("act", lambda e, c=c: e.activation(tA[:], tC[:], AF.Silu, scale=pcol(P_CLG + c), bias=pcol(P_CLB + c)),
                     reads=[tC, prm], writes=[tA])
                k.op("dve", lambda e, c=c: e.tensor_tensor(mixT[:, 4 + c, :], tA[:], gconv[:, c, :], op=ALU.mult),
                     reads=[tA, gconv], writes=[mixT])
            ck("convln")
            for blk in range(4):
                pu = [bank(), bank()]
                for kc in range(8):
                    k.op("pe", lambda e, kc=kc, blk=blk: e.matmul(pu[0][:, :], lhsT=hT[:, kc, blk * 128:(blk + 1) * 128],
                                                                  rhs=wbf[:, kc, SG:SG + 512], start=(kc == 0), stop=(kc == 7)),
                         reads=[wbf, hT], writes=[pu[0]])
                for kc in range(8):
                    k.op("pe", lambda e, kc=kc, blk=blk: e.matmul(pu[1][:, 0:256], lhsT=hT[:, kc, blk * 128:(blk + 1) * 128],
                                                                  rhs=wbf[:, kc, SG + 512:SG + 768], start=(kc == 0), stop=(kc == 7)),
                         reads=[wbf, hT], writes=[pu[1]])
                k.op("act", lambda e: e.activation(gu[:], pu[0][:, 0:256], AF.Gelu), reads=[pu[0]], writes=[gu])
                k.op("act", lambda e: e.activation(gv[:], pu[0][:, 256:512], AF.Gelu, accum_out=st[:, 0:1]), reads=[pu[0]], writes=[gv, st])
                k.op("act", lambda e: e.activation(gg[:], pu[1][:, 0:256], AF.Silu), reads=[pu[1]], writes=[gg])
                k.op("act", lambda e: e.activation(tC[:, 0:256], gv[:], AF.Square, accum_out=st[:, 1:2]), reads=[gv], writes=[tC, st])
                k.op("dve", lambda e: e.tensor_scalar(st[:, 2:4], st[:, 0:2], 1.0 / 256, None, op0=ALU.mult), reads=[st], writes=[st])
                k.op("dve", lambda e: e.tensor_tensor(st[:, 4:5], st[:, 2:3], st[:, 2:3], op=ALU.mult), reads=[st], writes=[st])
                k.op("dve", lambda e: e.tensor_tensor(st[:, 5:6], st[:, 3:4], st[:, 4:5], op=ALU.subtract), reads=[st], writes=[st])
                k.op("act", lambda e: e.activation(st[:, 6:7], st[:, 5:6], AF.Sqrt, bias=pcol(P_EPS)), reads=[st, prm], writes=[st])
                k.op("dve", lambda e: e.reciprocal(st[:, 7:8], st[:, 6:7]), reads=[st], writes=[st])
                k.op("dve", lambda e: e.tensor_scalar(gv[:], gv[:], st[:, 2:3], st[:, 7:8], op0=ALU.subtract, op1=ALU.mult),
                     reads=[gv, st], writes=[gv])
                k.op("dve", lambda e: e.tensor_tensor(gv[:], gv[:], bc[:, B_SLG:B_SLG + 256], op=ALU.mult), reads=[gv, bc], writes=[gv])
                k.op("dve", lambda e: e.tensor_tensor(vln[:], gv[:], bc[:, B_SLB:B_SLB + 256], op=ALU.add), reads=[gv, bc], writes=[vln])
                pm = bank()
                for h in range(4):
                    k.op("pe", lambda e, h=h: e.matmul(pm[:, h * 64:(h + 1) * 64], lhsT=wsbf[:, h, :], rhs=vln[:, h * 64:(h + 1) * 64],
                                                       start=True, stop=True), reads=[wsbf, vln], writes=[pm])
                k.op("dve", lambda e: e.tensor_tensor(gv[:].rearrange("p (h d) -> p h d", h=4), pm[:, 0:256].rearrange("p (h d) -> p h d", h=4),
                                                      bc[:, B_SGB:B_SGB + 4].unsqueeze(2).to_broadcast([128, 4, 64]), op=ALU.add),
                     reads=[pm, bc], writes=[gv])
                k.op("dve", lambda e: e.tensor_tensor(gv[:], gv[:], gu[:], op=ALU.mult), reads=[gv, gu], writes=[gv])
                k.op("dve", lambda e, blk=blk: e.tensor_tensor(sgt[:, blk, :], gv[:], gg[:], op=ALU.mult), reads=[gv, gg], writes=[sgt])
            for c in range(2):
                pt = pT[c]
                for blk in range(4):
                    k.op("pe", lambda e, c=c, blk=blk: e.transpose(pt[:, blk * 128:(blk + 1) * 128], sgt[:, blk, c * 128:(c + 1) * 128], ident),
                         reads=[sgt, cbf], writes=[pt])
                k.op("act", lambda e, c=c: e.activation(mixT[:, 6 + c, :], pt[:, 0:512], AF.Copy), reads=[pt], writes=[mixT])

            ck("sgu")
            Sb = [S0, S1]
            St = [S0t, S1t]
            for c in range(4):
                def qk(kt):
                    s = Sb[kt % 2]
                    k.op("pe", lambda e: e.matmul(s[0][:, :], lhsT=KT[0:64, kt * 128:(kt + 1) * 128], rhs=QT[0:64, c, :], start=True, stop=True),
                         reads=[KT, QT], writes=[s[0]])
                    k.op("pe", lambda e: e.matmul(s[1][:, :], lhsT=KT[64:128, kt * 128:(kt + 1) * 128], rhs=QT[64:128, c, :], start=True, stop=True),
                         reads=[KT, QT], writes=[s[1]])
                qk(0)
                if NK > 1:
                    qk(1)
                for kt in range(NK):
                    s = Sb[kt % 2]
                    p = Pb[kt % 2]
                    k.op("act", lambda e: e.activation(p[:], St[kt % 2][:, :], AF.Exp, scale=0.125), reads=s, writes=[p])
                    if kt + 2 < NK:
                        qk(kt + 2)
                    k.op("pe", lambda e: e.matmul(OE[0:65, :], lhsT=VA[:, kt, 0, 0:65], rhs=p[:, 0:512], start=(kt == 0), stop=(kt == NK - 1)),
                         reads=[VA, p], writes=[OE])
                    k.op("pe", lambda e: e.matmul(OO[0:65, :], lhsT=VA[:, kt, 1, 0:65], rhs=p[:, 512:1024], start=(kt == 0), stop=(kt == NK - 1)),
                         reads=[VA, p], writes=[OO])
                k.op("act", lambda e: e.activation(Osb[0:65, 0:512], OE[0:65, :], AF.Copy), reads=[OE], writes=[Osb])
                k.op("dve", lambda e: e.tensor_copy(Osb[0:65, 512:1024], OO[0:65, :]), reads=[OO], writes=[Osb])
                k.op("dve", lambda e: e.reciprocal(Osb[64:65, :], Osb[64:65, :]), reads=[Osb], writes=[Osb])
                k.op("pe", lambda e: e.matmul(pBc[0][0:64, :], lhsT=onesf[64:65, 0:64], rhs=Osb[64:65, 0:512], start=True, stop=True),
                     reads=[onesf, Osb], writes=[pBc[0]])
                k.op("pe", lambda e: e.matmul(pBc[1][0:64, :], lhsT=onesf[64:65, 0:64], rhs=Osb[64:65, 512:1024], start=True, stop=True),
                     reads=[onesf, Osb], writes=[pBc[1]])
                k.op("dve", lambda e: e.tensor_tensor(tmpA[0:64, :], Osb[0:64, 0:512], pBc[0][0:64, :], op=ALU.mult), reads=[Osb, pBc[0]], writes=[tmpA])
                k.op("dve", lambda e: e.tensor_tensor(tmpB[0:64, :], Osb[0:64, 512:1024], pBc[1][0:64, :], op=ALU.mult), reads=[Osb, pBc[1]], writes=[tmpB])
                k.dma("sp", "fin", [(tmpA[64:128, :], tmpB[0:64, :])], reads=[tmpB], writes=[tmpA])
                k.op("dve", lambda e, c=c: e.tensor_tensor(mixT[:, c, :], tmpA[:], gatt[:, c, :], op=ALU.mult), reads=[tmpA, gatt], writes=[mixT])

            if dbg and "mixT" in dbg and i == 0:
                k.dma("sp", "dbg", [(dbg_d["mixT"].rearrange("(c p) t -> p c t", p=128), mixT[:])], reads=[mixT])

            ck("att")
            for blk in range(4):
                sbuf = Sb[blk % 2]
                stt = St[blk % 2]
                for half in range(2):
                    for mc in range(8):
                        k.op("pe", lambda e, mc=mc, half=half: e.matmul(sbuf[half][:, :], lhsT=mixT[:, mc, blk * 128:(blk + 1) * 128],
                                                                        rhs=wobf[:, mc, half * 512:(half + 1) * 512], start=(mc == 0), stop=(mc == 7)),
                             reads=[mixT, wobf], writes=[sbuf[half]])
                xi = xslot[0]
                xslot[0] += 1
                xr = xs[xi % 2]
                k.dma("sp", f"xs{xi % 2}", [(xr[:], x_own[r0 + HALO + blk * 128:r0 + HALO + (blk + 1) * 128, :])], writes=[xr])
                k.op("act", lambda e: e.activation(yb[:], stt[:, :], AF.Square, accum_out=st[:, 8:9]), reads=sbuf, writes=[yb, st])
                k.op("dve", lambda e: e.tensor_scalar(st[:, 9:10], st[:, 8:9], 1.0 / D, EPS, op0=ALU.mult, op1=ALU.add), reads=[st], writes=[st])
                k.op("act", lambda e: e.activation(st[:, 10:11], st[:, 9:10], AF.Sqrt), reads=[st], writes=[st])
                k.op("dve", lambda e: e.reciprocal(st[:, 11:12], st[:, 10:11]), reads=[st], writes=[st])
                k.op("dve", lambda e: e.scalar_tensor_tensor(yb[:], stt[:, :], st[:, 11:12], bc[:, B_POST:B_POST + D], op0=ALU.mult, op1=ALU.mult),
                     reads=sbuf + [st, bc], writes=[yb])
                k.op("dve", lambda e: e.tensor_tensor(yb[:], yb[:], xr[:], op=ALU.add), reads=[yb, xr], writes=[yb])
                k.dma("sp", "yout", [(y_d[i * 512 + blk * 128:i * 512 + (blk + 1) * 128, :], yb[:])], reads=[yb])

    try:
        body()
    except _Stop:
        pass

    outb = Buf(None)
    for nm in ("yout", "dbg"):
        if nm in k.dsems:
            outb.d.lw = (k.dsems[nm], k.dcnt[nm])
            k.wait_all("sp", [outb])
    return nc


def _rope_tables(pos):
    pos = np.asarray(pos)
    inv = (10000.0 ** (-np.arange(16, dtype=np.float32) / 16)).astype(np.float32)
    row = (pos // 64).astype(np.float32)
    col = (pos % 64).astype(np.float32)
    d = np.arange(128) % 64
    p = np.where((d < 32)[:, None], row[None, :], col[None, :]).astype(np.float32)
    f = inv[(d % 32) % 16][:, None]
    ang = (p * f).astype(np.float32)
    return np.stack([np.cos(ang), np.sin(ang)]).astype(np.float32)


def _consts():
    c = np.zeros((128, 4, 128), np.float32)
    c[:, 0, :] = np.eye(128)
    for kk in range(128):
        if kk % 32 >= 16:
            c[kk, 1, kk - 16] = -1.0
        else:
            c[kk, 1, kk + 16] = 1.0
    for kk in range(128):
        c[kk, 2, (kk // 64) * 64:(kk // 64) * 64 + 64] = 1.0 / 64
    c[:, 3, :] = 1.0 / 256
    return c.astype(ml_dtypes.bfloat16)


def _layer_params(l, pre_norm, post_norm, w_in, w_out, q_norm, k_norm, conv_dw, conv_dw_b,
                  conv_ln_g, conv_ln_b, sg_ln_g, sg_ln_b, sg_w, sg_b):
    perm = np.arange(512).reshape(8, 64)
    hp = np.concatenate([np.concatenate([perm[c], perm[4 + c]]) for c in range(4)])
    cols = np.concatenate([hp, 512 + np.arange(256), 768 + hp, np.arange(1280, DIN)])
    wi = np.ascontiguousarray(w_in[l][:, cols])
    rows = np.concatenate([hp, np.arange(512, 1024)])
    wo = np.ascontiguousarray(w_out[l][rows, :])
    prm = np.zeros((128, P_N), np.float32)
    prm[:, P_GPRE:P_GPRE + 8] = pre_norm[l].reshape(8, 128).T
    prm[:, P_GQ] = np.tile(q_norm[l], 2)
    prm[:, P_GK] = np.tile(k_norm[l], 2)
    prm[:, P_EPS] = EPS
    prm[:, P_CB:P_CB + 2] = conv_dw_b[l].reshape(2, 128).T
    prm[:, P_CLG:P_CLG + 2] = conv_ln_g[l].reshape(2, 128).T
    prm[:, P_CLB:P_CLB + 2] = conv_ln_b[l].reshape(2, 128).T
    prm[:, P_DW:P_DW + 62] = conv_dw[l].T.reshape(2, 128, 31).transpose(1, 0, 2).reshape(128, 62)
    bcv = np.zeros((128, B_N), np.float32)
    bcv[:, B_POST:B_POST + D] = post_norm[l][None, :]
    bcv[:, B_SLG:B_SLG + 256] = sg_ln_g[l][None, :]
    bcv[:, B_SLB:B_SLB + 256] = sg_ln_b[l][None, :]
    bcv[:, B_SGB:B_SGB + 4] = sg_b[l].T
    wst = np.ascontiguousarray(sg_w[l].transpose(2, 0, 1))
    return {"w_in": wi, "w_out": wo, "prm": prm, "bc": bcv, "wst": wst}


_NC_CACHE = {}


def run_layer(x, l, params, dbg=None, trace=False):
    B, S, _ = x.shape
    per = NCORES // B
    TOWN = S // per
    key = (TOWN, S, tuple(sorted(dbg.items())) if dbg else None)
    nc = build_layer(TOWN, S, dbg)
    lp = _layer_params(l, **params)
    cbf = _consts()
    ropek = _rope_tables(np.arange(S))
    in_maps = []
    for c in range(NCORES):
        b, r = divmod(c, per)
        t0 = r * TOWN
        xo = np.zeros((TOWN + 2 * HALO, D), np.float32)
        lo, hi = max(0, t0 - HALO), min(S, t0 + TOWN + HALO)
        xo[lo - (t0 - HALO):hi - (t0 - HALO)] = x[b, lo:hi]
        m = {"x_all": np.ascontiguousarray(x[b]), "x_own": xo, "ropek": ropek,
             "ropeq": _rope_tables(np.arange(t0, t0 + TOWN)), "cbf": cbf}
        m.update(lp)
        in_maps.append(m)
    res = run_bass_kernel_spmd(nc, in_maps, core_ids=list(range(NCORES)), trace=trace)
    y = np.empty_like(x)
    for c in range(NCORES):
        b, r = divmod(c, per)
        y[b, r * TOWN:(r + 1) * TOWN] = res.results[c]["y"]
    return y, res


def kernel(x, pre_norm, post_norm, w_in, w_out, q_norm, k_norm, conv_dw, conv_dw_b,
           conv_ln_g, conv_ln_b, sg_ln_g, sg_ln_b, sg_w, sg_b):
    params = dict(pre_norm=pre_norm, post_norm=post_norm, w_in=w_in, w_out=w_out, q_norm=q_norm, k_norm=k_norm,
                  conv_dw=conv_dw, conv_dw_b=conv_dw_b, conv_ln_g=conv_ln_g, conv_ln_b=conv_ln_b,
                  sg_ln_g=sg_ln_g, sg_ln_b=sg_ln_b, sg_w=sg_w, sg_b=sg_b)
    params = {kk: np.asarray(v, np.float32) for kk, v in params.items()}
    x = np.asarray(x, np.float32)
    for l in range(pre_norm.shape[0]):
        x, _ = run_layer(x, l, params)
    return x
```

```python
import numpy as np
import ml_dtypes
import concourse.bass as bass
import concourse.mybir as mybir
from concourse.bass_utils import run_bass_kernel_spmd

F32 = mybir.dt.float32
BF16 = mybir.dt.bfloat16
AF = mybir.ActivationFunctionType
ALU = mybir.AluOpType

D = 1024
DIN = 2816
NCORES = 8
EPS = 1e-6
HALO = 16

QC, KC, VC, GA, CA, CB, GC, SG = 0, 512, 640, 768, 1280, 1536, 1792, 2048
P_GPRE, P_GQ, P_GK, P_EPS, P_CB, P_CLG, P_CLB, P_DW, P_N = 0, 8, 9, 10, 11, 13, 15, 17, 17 + 62
B_POST, B_SLG, B_SLB, B_SGB, B_N = 0, 1024, 1280, 1536, 1540


class Dep:
    __slots__ = ("lw", "rd")

    def __init__(self):
        self.lw = None
        self.rd = {}


class Buf:
    __slots__ = ("t", "d", "name", "psum")

    def __init__(self, t, name="", d=None, psum=False):
        self.t = t
        self.d = d if d is not None else Dep()
        self.name = name
        self.psum = psum

    def __getitem__(self, idx):
        return self.t[idx]

    def alias(self, ap, name=""):
        return Buf(ap, name, self.d, self.psum)


class K:
    def __init__(self, nc, same_engine_sync=True):
        self.nc = nc
        self.eng = {"pe": nc.tensor, "act": nc.scalar, "dve": nc.vector, "pool": nc.gpsimd, "sp": nc.sync}
        self.sem = {e: nc.alloc_semaphore("prog_" + e) for e in self.eng}
        self.cnt = {e: 0 for e in self.eng}
        self.seen = {e: {} for e in self.eng}
        self.same = same_engine_sync
        self.nbuf = 0
        self.dsems = {}
        self.dcnt = {}

    def sb(self, shape, dt, name=None):
        self.nbuf += 1
        name = "s_" + (name or f"sb{self.nbuf}")
        return Buf(self.nc.alloc_sbuf_tensor(name, list(shape), dt), name)

    def _deps(self, e, reads, writes, attach_ok=False):
        need = {}
        own = self.sem[e]
        for b in reads:
            if b.d.lw is not None:
                s, v = b.d.lw
                if need.get(s, 0) < v:
                    need[s] = v
            if b.psum:
                for s, v in b.d.rd.items():
                    if s is not own and need.get(s, 0) < v:
                        need[s] = v
        for b in writes:
            if b.d.lw is not None:
                s, v = b.d.lw
                if need.get(s, 0) < v:
                    need[s] = v
            for s, v in b.d.rd.items():
                if need.get(s, 0) < v:
                    need[s] = v
        own = self.sem[e]
        seen = self.seen[e]
        todo = []
        for s, v in need.items():
            if s is own and (e == "pe" or not self.same):
                continue
            if seen.get(s, 0) >= v:
                continue
            todo.append((s, v))
            seen[s] = v
        attach = todo.pop() if (todo and attach_ok) else None
        for s, v in todo:
            self.eng[e].wait_ge(s, v)
        return attach

    def _done(self, tok, reads, writes):
        s, v = tok
        for b in writes:
            b.d.lw = tok
            b.d.rd = {}
        for b in reads:
            if b.d.rd.get(s, 0) < v:
                b.d.rd[s] = v

    def op(self, e, fn, reads=(), writes=()):
        att = self._deps(e, reads, writes, attach_ok=True)
        ins = fn(self.eng[e])
        if att is not None:
            ins._wait_ge(att[0], att[1])
        self.cnt[e] += 1
        ins.then_inc(self.sem[e], 1)
        self._done((self.sem[e], self.cnt[e]), reads, writes)
        return ins

    def dma(self, q, semname, pairs, reads=(), writes=()):
        if semname not in self.dsems:
            self.dsems[semname] = self.nc.alloc_semaphore("dma_" + semname)
            self.dcnt[semname] = 0
        s = self.dsems[semname]
        self._deps(q, reads, writes)
        for o, i in pairs:
            self.eng[q].dma_start(out=o, in_=i).then_inc(s, 16)
            self.dcnt[semname] += 16
        self._done((s, self.dcnt[semname]), reads, writes)

    def wait_all(self, e, bufs):
        self._deps(e, bufs, [])


class _Stop(Exception):
    pass


def build_fused(TOWN, SKV, depth, dbg=None, stop=None):
    RPG = SKV // TOWN
    NT = TOWN // 512
    NKT = SKV // 512
    NK = SKV // 128
    nc = bass.Bass("TRN2", target_bir_lowering=False)
    k = K(nc)

    def dram(name, shape, dt=F32, kind="ExternalInput"):
        return nc.dram_tensor(name, list(shape), dt, kind=kind).ap()

    x_own = dram("x_own", [TOWN + 2 * HALO, D])
    w_in_d = dram("w_in", [depth, D, DIN])
    w_out_d = dram("w_out", [depth, D, D])
    ropeq = dram("ropeq", [2, 128, TOWN])
    cbf_d = dram("cbf", [128, 4, 128], BF16)
    prm_dd = dram("prm", [depth, 128, P_N])
    bc_dd = dram("bc", [depth, 128, B_N])
    wst_dd = dram("wst", [depth, 128, 4, 128])
    sel_d = dram("sel", [128, 32])
    y_d = dram("y", [TOWN, D], kind="ExternalOutput")
    NKO = TOWN // 128
    kt_loc = [Buf(dram(f"kt_loc{j}", [128, 512], BF16, kind="Internal"), f"kt_loc{j}") for j in range(NT)]
    va_loc = [Buf(dram(f"va_loc{j}", [128, 528], BF16, kind="Internal"), f"va_loc{j}") for j in range(NT)]
    kt_all = [Buf(dram(f"kt_all{j}", [RPG * 128, 512], BF16, kind="Internal"), f"kt_all{j}") for j in range(NT)]
    va_all = [Buf(dram(f"va_all{j}", [RPG * 128, 528], BF16, kind="Internal"), f"va_all{j}") for j in range(NT)]
    hb_loc = Buf(dram("hb_loc", [32, D], F32, kind="Internal"), "hb_loc")
    hb_all = Buf(dram("hb_all", [RPG * 32, D], F32, kind="Internal"), "hb_all")
    x1 = Buf(dram("x1", [TOWN + 2 * HALO, D], F32, kind="Internal"), "x1")
    xin = Buf(x_own, "x_own")
    groups = [list(range(g * RPG, (g + 1) * RPG)) for g in range(NCORES // RPG)]
    cc_sem = nc.alloc_semaphore("cc_sem")
    cc_cnt = [0]
    dbg_d = {}
    if dbg:
        for nm, shp in dbg.items():
            dbg_d[nm] = dram("dbg_" + nm, shp, kind="ExternalOutput")

    KT = k.sb([128, SKV], BF16, "KT")
    VA = k.sb([128, NK, 2, 66], BF16, "VA")
    wbf = k.sb([128, 8, DIN], BF16, "wbf")
    wobf = k.sb([128, 8, D], BF16, "wobf")
    wsbf = k.sb([128, 4, 128], BF16, "wsbf")
    cbf = k.sb([128, 4, 128], BF16, "cbf")
    prm = k.sb([128, P_N], F32, "prm")
    bc = k.sb([128, B_N], F32, "bc")
    xs = [k.sb([128, D], F32, f"xs{i}") for i in range(2)]
    xbs = [k.sb([128, D], BF16, f"xb{i}") for i in range(2)]
    yb = k.sb([128, D], F32, "yb")
    hT = k.sb([128, 8, 544], BF16, "hT")
    cosb = k.sb([128, 512], F32, "cosb")
    sinb = k.sb([128, 512], F32, "sinb")
    tA = k.sb([128, 512], F32, "tA")
    tB = k.sb([128, 512], F32, "tB")
    tC = k.sb([128, 512], F32, "tC")
    sqb = k.sb([128, 512], BF16, "sqb")
    qgb = k.sb([128, 512], BF16, "qgb")
    QT = k.sb([128, 4, 512], BF16, "QT")
    gatt = k.sb([128, 4, 512], BF16, "gatt")
    hpad = k.sb([128, 2, 544], F32, "hpad")
    acc = k.sb([128, 2, 512], F32, "acc")
    gconv = k.sb([128, 2, 512], BF16, "gconv")
    vln = k.sb([128, 256], BF16, "vln")
    mixT = k.sb([128, 8, 512], BF16, "mixT")
    Pb = [k.sb([128, 1024], BF16, f"P{i}") for i in range(2)]
    st = k.sb([128, 32], F32, "st")
    cb16 = Pb[0].alias(Pb[0][:, :].rearrange("p (a b) -> p a b", a=2), "cb16")
    csq16 = Pb[1].alias(Pb[1][:, :].rearrange("p (a b) -> p a b", a=2), "csq16")
    sgt = gconv.alias(gconv[:, :, :].rearrange("p a (c d) -> p (a c) d", c=2), "sgt")
    Osb = acc.alias(acc[:, :, :].rearrange("p a b -> p (a b)"), "Osb")
    gu = tA.alias(tA[:, 0:256], "gu")
    gv = tA.alias(tA[:, 256:512], "gv")
    gg = tB.alias(tB[:, 0:256], "gg")
    tmpA = qgb
    tmpB = sqb
    onesf = k.sb([128, 64], F32, "onesf")
    self_sel = k.sb([128, 32], F32, "sel")

    S0t = nc.alloc_psum_tensor("S0", [128, 1024], F32)
    S1t = nc.alloc_psum_tensor("S1", [128, 1024], F32)
    OEt = nc.alloc_psum_tensor("OE", [128, 512], F32)
    OOt = nc.alloc_psum_tensor("OO", [128, 512], F32)
    pT = [Buf(nc.alloc_psum_tensor(f"pT{i}", [128, 1024], BF16), f"pT{i}", psum=True) for i in range(2)]
    S0 = [Buf(S0t[:, 0:512], "S0a", psum=True), Buf(S0t[:, 512:1024], "S0b", psum=True)]
    S1 = [Buf(S1t[:, 0:512], "S1a", psum=True), Buf(S1t[:, 512:1024], "S1b", psum=True)]
    OE = Buf(OEt[:, :], "OE", psum=True)
    OO = Buf(OOt[:, :], "OO", psum=True)
    pBc = [pT[i].alias(pT[i][:, :].bitcast(F32), f"pBc{i}") for i in range(2)]
    banks = [OE, OO, S0[0], S0[1], S1[0], S1[1]]
    bank_i = [0]

    def bank():
        b = banks[bank_i[0] % len(banks)]
        bank_i[0] += 1
        return b

    ident = cbf[:, 0, :]
    RT = cbf[:, 1, :]
    bones = cbf[:, 2, :]
    o256 = cbf[:, 3, :]

    def pcol(c, rows=128):
        return prm[0:rows, c:c + 1]

    def ck(name):
        if stop == name:
            raise _Stop()

    def body():
        k.dma("sp", "c0", [(cbf[:], cbf_d[:, :, :]), (self_sel[:], sel_d[:, :])], writes=[cbf, self_sel])
        k.op("dve", lambda e: e.memset(st[:], 0.0), writes=[st])
        k.op("dve", lambda e: e.memset(onesf[:], 1.0), writes=[onesf])
        xslot = [0]

        def load_weights(l):
          w_in = w_in_d[l]
          w_out = w_out_d[l]
          wst_d = wst_dd[l]
          k.dma("sp", "c0", [(prm[:], prm_dd[l]), (bc[:], bc_dd[l])], writes=[prm, bc])
          si = xslot[0]
          s = xs[si % 2]
          k.dma("sp", f"xs{si % 2}", [(s[:, 0:512], wst_d.rearrange("q h p -> q (h p)"))], writes=[s])
          k.op("dve", lambda e: e.tensor_copy(wsbf[:].rearrange("q h p -> q (h p)"), s[:, 0:512]), reads=[s], writes=[wsbf])
          si += 1
          for kc in range(8):
            for (c0, c1) in ((0, 1024), (1024, 2048), (2048, DIN)):
                s = xs[si % 2]
                k.dma("sp", f"xs{si % 2}", [(s[:, 0:c1 - c0], w_in[kc * 128:(kc + 1) * 128, c0:c1])], writes=[s])
                if si % 2:
                    k.op("dve", lambda en, s=s, kc=kc, c0=c0, c1=c1: en.tensor_scalar(
                        wbf[:, kc, c0:c1], s[:, 0:c1 - c0], pcol(P_GPRE + kc), None, op0=ALU.mult), reads=[s, prm], writes=[wbf])
                else:
                    k.op("act", lambda en, s=s, kc=kc, c0=c0, c1=c1: en.activation(
                        wbf[:, kc, c0:c1], s[:, 0:c1 - c0], AF.Copy, scale=pcol(P_GPRE + kc)), reads=[s, prm], writes=[wbf])
                si += 1
          for kc in range(8):
            s = xs[si % 2]
            k.dma("sp", f"xs{si % 2}", [(s[:], w_out[kc * 128:(kc + 1) * 128, :])], writes=[s])
            if si % 2:
                k.op("dve", lambda en, s=s, kc=kc: en.tensor_copy(wobf[:, kc, :], s[:]), reads=[s], writes=[wobf])
            else:
                k.op("act", lambda en, s=s, kc=kc: en.activation(wobf[:, kc, :], s[:], AF.Copy), reads=[s], writes=[wobf])
            si += 1
          xslot[0] = si

        def load_norm_T(srcb, row0, nrows, col0):
            src = srcb.t
            i = xslot[0]
            xslot[0] += 1
            s, xb, pt = xs[i % 2], xbs[i % 2], pT[i % 2]
            c = i % 8
            k.dma("sp", f"xs{i % 2}", [(s[0:nrows, :], src[row0:row0 + nrows, :])], reads=[srcb], writes=[s])
            k.op("act", lambda e: e.activation(yb[0:nrows, :], s[0:nrows, :], AF.Square, accum_out=st[0:nrows, c:c + 1]),
                 reads=[s], writes=[yb, st])
            k.op("dve", lambda e: e.tensor_scalar(st[0:nrows, 8 + c:9 + c], st[0:nrows, c:c + 1], 1.0 / D, EPS,
                                                  op0=ALU.mult, op1=ALU.add), reads=[st], writes=[st])
            k.op("act", lambda e: e.activation(st[0:nrows, 16 + c:17 + c], st[0:nrows, 8 + c:9 + c], AF.Sqrt), reads=[st], writes=[st])
            k.op("dve", lambda e: e.reciprocal(st[0:nrows, 24 + c:25 + c], st[0:nrows, 16 + c:17 + c]), reads=[st], writes=[st])
            k.op("dve", lambda e: e.tensor_scalar(xb[0:nrows, :], s[0:nrows, :], st[0:nrows, 24 + c:25 + c], None, op0=ALU.mult),
                 reads=[s, st], writes=[xb])
            for kc in range(8):
                k.op("pe", lambda e, kc=kc: e.transpose(pt[:, kc * 128:kc * 128 + nrows], xb[0:nrows, kc * 128:(kc + 1) * 128],
                                                        ident[0:nrows, 0:nrows]), reads=[xb, cbf], writes=[pt])
            src_ap = pt[:, :].rearrange("p (a b) -> p a b", a=8)[:, :, 0:nrows]
            k.op("act", lambda e: e.activation(hT[:, :, col0:col0 + nrows], src_ap, AF.Copy), reads=[pt], writes=[hT])

        def proj_fm(col0, n0=0, n=512, m=128):
            b = bank()
            for kc in range(8):
                k.op("pe", lambda e, kc=kc: e.matmul(b[0:m, 0:n], lhsT=wbf[:, kc, col0:col0 + m], rhs=hT[:, kc, n0:n0 + n],
                                                     start=(kc == 0), stop=(kc == 7)), reads=[wbf, hT], writes=[b])
            return b

        def norm_rope(pP, gcol, out_ap, out_buf):
            k.op("act", lambda e: e.activation(sqb[:], pP[:, :], AF.Square), reads=[pP], writes=[sqb])
            k.op("dve", lambda e: e.tensor_scalar(qgb[:], pP[:, :], pcol(gcol), None, op0=ALU.mult), reads=[pP, prm], writes=[qgb])
            k.op("dve", lambda e: e.scalar_tensor_tensor(tA[:], pP[:, :], pcol(gcol), cosb[:], op0=ALU.mult, op1=ALU.mult),
                 reads=[pP, prm, cosb], writes=[tA])
            pMS = bank()
            k.op("pe", lambda e: e.matmul(pMS[:, :], lhsT=bones, rhs=sqb[:], start=True, stop=True), reads=[cbf, sqb], writes=[pMS])
            pRO = bank()
            k.op("pe", lambda e: e.matmul(pRO[:, :], lhsT=RT, rhs=qgb[:], start=True, stop=True), reads=[cbf, qgb], writes=[pRO])
            k.op("act", lambda e: e.activation(tB[:], pMS[:, :], AF.Sqrt, bias=pcol(P_EPS)), reads=[pMS, prm], writes=[tB])
            k.op("dve", lambda e: e.reciprocal(tB[:], tB[:]), reads=[tB], writes=[tB])
            k.op("dve", lambda e: e.tensor_tensor(tC[:], pRO[:, :], sinb[:], op=ALU.mult), reads=[pRO, sinb], writes=[tC])
            k.op("dve", lambda e: e.tensor_tensor(tA[:], tA[:], tC[:], op=ALU.add), reads=[tA, tC], writes=[tA])
            k.op("dve", lambda e: e.tensor_tensor(out_ap, tA[:], tB[:], op=ALU.mult), reads=[tA, tB], writes=[out_buf])

        def load_rope(tab, t0):
            k.dma("sp", "rope", [(cosb[:], tab[0, :, t0:t0 + 512]), (sinb[:], tab[1, :, t0:t0 + 512])], writes=[cosb, sinb])

        def allgather(pairs):
            for srcb, dstb in pairs:
                k._deps("pool", [srcb], [dstb])
                nc.gpsimd.collective_compute("AllGather", ALU.bypass, replica_groups=groups,
                                             ins=[srcb[:, :]], outs=[dstb[:, :]]).then_inc(cc_sem, 1)
                cc_cnt[0] += 1
                k._done((cc_sem, cc_cnt[0]), [srcb], [dstb])

        def halo_exchange():
            allgather([(hb_loc, hb_all)])
            i = xslot[0]
            xslot[0] += 1
            s = xs[i % 2]
            k.dma("sp", f"xs{i % 2}", [(s[0:RPG * 32, :], hb_all[:, :])], reads=[hb_all], writes=[s])
            for half in range(2):
                b_ = S0[half]
                k.op("pe", lambda e, half=half, b_=b_: e.matmul(b_[0:32, :], lhsT=self_sel[0:RPG * 32, :], rhs=s[0:RPG * 32, half * 512:(half + 1) * 512],
                                                               start=True, stop=True), reads=[self_sel, s], writes=[b_])
                k.op("dve", lambda e, half=half, b_=b_: e.tensor_copy(yb[0:32, half * 512:(half + 1) * 512], b_[0:32, :]), reads=[b_], writes=[yb])
            k.dma("sp", "yout", [(x1[0:HALO, :], yb[0:HALO, :]), (x1[HALO + TOWN:2 * HALO + TOWN, :], yb[HALO:2 * HALO, :])],
                  reads=[yb], writes=[x1])

        def layer(l, xsrc, last):
            load_weights(l)
            ck("setup")
            for j in range(NT):
                load_rope(ropeq, j * 512)
                ck("k_rope")
                for blk in range(4):
                    load_norm_T(xsrc, HALO + j * 512 + blk * 128, 128, blk * 128)
                    ck("k_lnt1")
                pK = proj_fm(KC)
                ck("k_proj")
                norm_rope(pK, P_GK, KT[:, j * 512:(j + 1) * 512], KT)
                ck("k_nr")
                pV = bank()
                for blk in range(4):
                    for kc in range(8):
                        k.op("pe", lambda e, kc=kc, blk=blk: e.matmul(pV[:, blk * 128:(blk + 1) * 128], lhsT=hT[:, kc, blk * 128:(blk + 1) * 128],
                                                                      rhs=wbf[:, kc, VC:VC + 128], start=(kc == 0), stop=(kc == 7)),
                             reads=[wbf, hT], writes=[pV])
                ck("k_vmm")
                k.op("act", lambda e: e.activation(VA[:, j * 4:(j + 1) * 4, :, 0:64],
                                                   pV[:, :].rearrange("p (a g d) -> p a g d", a=4, g=2), AF.Copy), reads=[pV], writes=[VA])

            for j in range(NT):
                k.dma("sp", "xch", [(kt_loc[j][:, :], KT[:, j * 512:(j + 1) * 512]),
                                    (va_loc[j][:, :], VA[:, j * 4:(j + 1) * 4, :, :].rearrange("p a g d -> p (a g d)"))],
                      reads=[KT, VA], writes=[kt_loc[j], va_loc[j]])
            for j in range(NT):
                allgather([(kt_loc[j], kt_all[j]), (va_loc[j], va_all[j])])
            pairs = []
            for j in range(NT):
                for r in range(RPG):
                    pairs.append((KT[:, r * TOWN + j * 512:r * TOWN + (j + 1) * 512], kt_all[j][r * 128:(r + 1) * 128, :]))
                    pairs.append((VA[:, r * NKO + j * 4:r * NKO + (j + 1) * 4, :, :].rearrange("p a g d -> p (a g d)"),
                                  va_all[j][r * 128:(r + 1) * 128, :]))
            k.dma("sp", "xch", pairs, reads=kt_all + va_all, writes=[KT, VA])

            ck("phaseK")
            for i in range(NT):
                r0 = i * 512
                load_rope(ropeq, i * 512)
                for blk in range(4):
                    load_norm_T(xsrc, r0 + HALO + blk * 128, 128, blk * 128)
                load_norm_T(xsrc, r0, HALO, 512)
                load_norm_T(xsrc, r0 + HALO + 512, HALO, 528)

                for c in range(4):
                    pq = proj_fm(QC + c * 128)
                    norm_rope(pq, P_GQ, QT[:, c, :], QT)
                ck("qnorm")
                for c in range(4):
                    pg = proj_fm(GA + c * 128)
                    k.op("act", lambda e, c=c, pg=pg: e.activation(gatt[:, c, :], pg[:, :], AF.Silu), reads=[pg], writes=[gatt])
                ck("gates")
                for c in range(2):
                    pa = proj_fm(CA + c * 128)
                    pb = proj_fm(CB + c * 128)
                    k.op("act", lambda e, pb=pb: e.activation(tB[:], pb[:, :], AF.Sigmoid), reads=[pb], writes=[tB])
                    k.op("dve", lambda e, c=c, pa=pa: e.tensor_tensor(hpad[:, c, HALO:HALO + 512], pa[:, :], tB[:], op=ALU.mult),
                         reads=[pa, tB], writes=[hpad])
                    pa2 = proj_fm(CA + c * 128, n0=512, n=32)
                    pb2 = proj_fm(CB + c * 128, n0=512, n=32)
                    k.op("act", lambda e, pb2=pb2: e.activation(tC[:, 0:32], pb2[:, 0:32], AF.Sigmoid), reads=[pb2], writes=[tC])
                    k.op("dve", lambda e, c=c, pa2=pa2: e.tensor_tensor(hpad[:, c, 0:HALO], pa2[:, 0:HALO], tC[:, 0:HALO], op=ALU.mult),
                         reads=[pa2, tC], writes=[hpad])
                    k.op("dve", lambda e, c=c, pa2=pa2: e.tensor_tensor(hpad[:, c, HALO + 512:544], pa2[:, HALO:32], tC[:, HALO:32], op=ALU.mult),
                         reads=[pa2, tC], writes=[hpad])
                for c in range(2):
                    pg = proj_fm(GC + c * 128)
                    k.op("act", lambda e, c=c, pg=pg: e.activation(gconv[:, c, :], pg[:, :], AF.Silu), reads=[pg], writes=[gconv])
                ck("glu")
                for c in range(2):
                    eng = "dve"
                    k.op(eng, lambda e, c=c: e.tensor_scalar(acc[:, c, :], hpad[:, c, 1:513], pcol(P_DW + c * 31), pcol(P_CB + c),
                                                             op0=ALU.mult, op1=ALU.add), reads=[hpad, prm], writes=[acc])
                    for j in range(1, 31):
                        k.op(eng, lambda e, c=c, j=j: e.scalar_tensor_tensor(acc[:, c, :], hpad[:, c, 1 + j:513 + j], pcol(P_DW + c * 31 + j),
                                                                             acc[:, c, :], op0=ALU.mult, op1=ALU.add),
                             reads=[hpad, prm, acc], writes=[acc])
                ck("conv")
                for c in range(2):
                    k.op("act", lambda e, c=c: e.activation(cb16[:, c, :], acc[:, c, :], AF.Copy), reads=[acc], writes=[cb16])
                    k.op("act", lambda e, c=c: e.activation(csq16[:, c, :], acc[:, c, :], AF.Square), reads=[acc], writes=[csq16])
                pM1 = bank()
                pM2 = bank()
                for c in range(2):
                    k.op("pe", lambda e, c=c: e.matmul(pM1[:, :], lhsT=o256, rhs=cb16[:, c, :], start=(c == 0), stop=(c == 1)),
                         reads=[cbf, cb16], writes=[pM1])
                for c in range(2):
                    k.op("pe", lambda e, c=c: e.matmul(pM2[:, :], lhsT=o256, rhs=csq16[:, c, :], start=(c == 0), stop=(c == 1)),
                         reads=[cbf, csq16], writes=[pM2])
                k.op("act", lambda e: e.activation(tA[:], pM1[:, :], AF.Square), reads=[pM1], writes=[tA])
                k.op("dve", lambda e: e.tensor_tensor(tA[:], pM2[:, :], tA[:], op=ALU.subtract), reads=[pM2, tA], writes=[tA])
                k.op("act", lambda e: e.activation(tB[:], tA[:], AF.Sqrt, bias=pcol(P_EPS)), reads=[tA, prm], writes=[tB])
                k.op("dve", lambda e: e.reciprocal(tB[:], tB[:]), reads=[tB], writes=[tB])
                for c in range(2):
                    k.op("dve", lambda e, c=c: e.tensor_tensor(tC[:], acc[:, c, :], pM1[:, :], op=ALU.subtract), reads=[acc, pM1], writes=[tC])
                    k.op("dve", lambda e: e.tensor_tensor(tC[:], tC[:], tB[:], op=ALU.mult), reads=[tC, tB], writes=[tC])
                    k.op("act", lambda e, c=c: e.activation(tA[:], tC[:], AF.Silu, scale=pcol(P_CLG + c), bias=pcol(P_CLB + c)),
                         reads=[tC, prm], writes=[tA])
                    k.op("dve", lambda e, c=c: e.tensor_tensor(mixT[:, 4 + c, :], tA[:], gconv[:, c, :], op=ALU.mult),
                         reads=[tA, gconv], writes=[mixT])
                ck("convln")
                for blk in range(4):
                    pu = [bank(), bank()]
                    for kc in range(8):
                        k.op("pe", lambda e, kc=kc, blk=blk: e.matmul(pu[0][:, :], lhsT=hT[:, kc, blk * 128:(blk + 1) * 128],
                                                                      rhs=wbf[:, kc, SG:SG + 512], start=(kc == 0), stop=(kc == 7)),
                             reads=[wbf, hT], writes=[pu[0]])
                    for kc in range(8):
                        k.op("pe", lambda e, kc=kc, blk=blk: e.matmul(pu[1][:, 0:256], lhsT=hT[:, kc, blk * 128:(blk + 1) * 128],
                                                                      rhs=wbf[:, kc, SG + 512:SG + 768], start=(kc == 0), stop=(kc == 7)),
                             reads=[wbf, hT], writes=[pu[1]])
                    k.op("act", lambda e: e.activation(gu[:], pu[0][:, 0:256], AF.Gelu), reads=[pu[0]], writes=[gu])
                    k.op("act", lambda e: e.activation(gv[:], pu[0][:, 256:512], AF.Gelu, accum_out=st[:, 0:1]), reads=[pu[0]], writes=[gv, st])
                    k.op("act", lambda e: e.activation(gg[:], pu[1][:, 0:256], AF.Silu), reads=[pu[1]], writes=[gg])
                    k.op("act", lambda e: e.activation(tC[:, 0:256], gv[:], AF.Square, accum_out=st[:, 1:2]), reads=[gv], writes=[tC, st])
                    k.op("dve", lambda e: e.tensor_scalar(st[:, 2:4], st[:, 0:2], 1.0 / 256, None, op0=ALU.mult), reads=[st], writes=[st])
                    k.op("dve", lambda e: e.tensor_tensor(st[:, 4:5], st[:, 2:3], st[:, 2:3], op=ALU.mult), reads=[st], writes=[st])
                    k.op("dve", lambda e: e.tensor_tensor(st[:, 5:6], st[:, 3:4], st[:, 4:5], op=ALU.subtract), reads=[st], writes=[st])
                    k.op("act", lambda e: e.activation(st[:, 6:7], st[:, 5:6], AF.Sqrt, bias=pcol(P_EPS)), reads=[st, prm], writes=[st])
                    k.op("dve", lambda e: e.reciprocal(st[:, 7:8], st[:, 6:7]), reads=[st], writes=[st])
                    k.op("dve", lambda e: e.tensor_scalar(gv[:], gv[:], st[:, 2:3], st[:, 7:8], op0=ALU.subtract, op1=ALU.mult),
                         reads=[gv, st], writes=[gv])
                    k.op("dve", lambda e: e.tensor_tensor(gv[:], gv[:], bc[:, B_SLG:B_SLG + 256], op=ALU.mult), reads=[gv, bc], writes=[gv])
                    k.op("dve", lambda e: e.tensor_tensor(vln[:], gv[:], bc[:, B_SLB:B_SLB + 256], op=ALU.add), reads=[gv, bc], writes=[vln])
                    pm = bank()
                    for h in range(4):
                        k.op("pe", lambda e, h=h: e.matmul(pm[:, h * 64:(h + 1) * 64], lhsT=wsbf[:, h, :], rhs=vln[:, h * 64:(h + 1) * 64],
                                                           start=True, stop=True), reads=[wsbf, vln], writes=[pm])
                    k.op("dve", lambda e: e.tensor_tensor(gv[:].rearrange("p (h d) -> p h d", h=4), pm[:, 0:256].rearrange("p (h d) -> p h d", h=4),
                                                          bc[:, B_SGB:B_SGB + 4].unsqueeze(2).to_broadcast([128, 4, 64]), op=ALU.add),
                         reads=[pm, bc], writes=[gv])
                    k.op("dve", lambda e: e.tensor_tensor(gv[:], gv[:], gu[:], op=ALU.mult), reads=[gv, gu], writes=[gv])
                    k.op("dve", lambda e, blk=blk: e.tensor_tensor(sgt[:, blk, :], gv[:], gg[:], op=ALU.mult), reads=[gv, gg], writes=[sgt])
                for c in range(2):
                    pt = pT[c]
                    for blk in range(4):
                        k.op("pe", lambda e, c=c, blk=blk: e.transpose(pt[:, blk * 128:(blk + 1) * 128], sgt[:, blk, c * 128:(c + 1) * 128], ident),
                             reads=[sgt, cbf], writes=[pt])
                    k.op("act", lambda e, c=c: e.activation(mixT[:, 6 + c, :], pt[:, 0:512], AF.Copy), reads=[pt], writes=[mixT])

                ck("sgu")
                Sb = [S0, S1]
                St = [S0t, S1t]
                for c in range(4):
                    def qk(kt):
                        s = Sb[kt % 2]
                        k.op("pe", lambda e: e.matmul(s[0][:, :], lhsT=KT[0:64, kt * 128:(kt + 1) * 128], rhs=QT[0:64, c, :], start=True, stop=True),
                             reads=[KT, QT], writes=[s[0]])
                        k.op("pe", lambda e: e.matmul(s[1][:, :], lhsT=KT[64:128, kt * 128:(kt + 1) * 128], rhs=QT[64:128, c, :], start=True, stop=True),
                             reads=[KT, QT], writes=[s[1]])
                    qk(0)
                    if NK > 1:
                        qk(1)
                    for kt in range(NK):
                        s = Sb[kt % 2]
                        p = Pb[kt % 2]
                        k.op("act", lambda e: e.activation(p[:], St[kt % 2][:, :], AF.Exp, scale=0.125), reads=s, writes=[p])
                        if kt + 2 < NK:
                            qk(kt + 2)
                        k.op("pe", lambda e: e.matmul(OE[0:65, :], lhsT=VA[:, kt, 0, 0:65], rhs=p[:, 0:512], start=(kt == 0), stop=(kt == NK - 1)),
                             reads=[VA, p], writes=[OE])
                        k.op("pe", lambda e: e.matmul(OO[0:65, :], lhsT=VA[:, kt, 1, 0:65], rhs=p[:, 512:1024], start=(kt == 0), stop=(kt == NK - 1)),
                             reads=[VA, p], writes=[OO])
                    k.op("act", lambda e: e.activation(Osb[0:65, 0:512], OE[0:65, :], AF.Copy), reads=[OE], writes=[Osb])
                    k.op("dve", lambda e: e.tensor_copy(Osb[0:65, 512:1024], OO[0:65, :]), reads=[OO], writes=[Osb])
                    k.op("dve", lambda e: e.reciprocal(Osb[64:65, :], Osb[64:65, :]), reads=[Osb], writes=[Osb])
                    k.op("pe", lambda e: e.matmul(pBc[0][0:64, :], lhsT=onesf[64:65, 0:64], rhs=Osb[64:65, 0:512], start=True, stop=True),
                         reads=[onesf, Osb], writes=[pBc[0]])
                    k.op("pe", lambda e: e.matmul(pBc[1][0:64, :], lhsT=onesf[64:65, 0:64], rhs=Osb[64:65, 512:1024], start=True, stop=True),
                         reads=[onesf, Osb], writes=[pBc[1]])
                    k.op("dve", lambda e: e.tensor_tensor(tmpA[0:64, :], Osb[0:64, 0:512], pBc[0][0:64, :], op=ALU.mult), reads=[Osb, pBc[0]], writes=[tmpA])
                    k.op("dve", lambda e: e.tensor_tensor(tmpB[0:64, :], Osb[0:64, 512:1024], pBc[1][0:64, :], op=ALU.mult), reads=[Osb, pBc[1]], writes=[tmpB])
                    k.dma("sp", "fin", [(tmpA[64:128, :], tmpB[0:64, :])], reads=[tmpB], writes=[tmpA])
                    k.op("dve", lambda e, c=c: e.tensor_tensor(mixT[:, c, :], tmpA[:], gatt[:, c, :], op=ALU.mult), reads=[tmpA, gatt], writes=[mixT])

                if dbg and "mixT" in dbg and i == 0:
                    k.dma("sp", "dbg", [(dbg_d["mixT"].rearrange("(c p) t -> p c t", p=128), mixT[:])], reads=[mixT])

                ck("att")
                for blk in range(4):
                    sbuf = Sb[blk % 2]
                    stt = St[blk % 2]
                    for half in range(2):
                        for mc in range(8):
                            k.op("pe", lambda e, mc=mc, half=half: e.matmul(sbuf[half][:, :], lhsT=mixT[:, mc, blk * 128:(blk + 1) * 128],
                                                                            rhs=wobf[:, mc, half * 512:(half + 1) * 512], start=(mc == 0), stop=(mc == 7)),
                                 reads=[mixT, wobf], writes=[sbuf[half]])
                    xi = xslot[0]
                    xslot[0] += 1
                    xr = xs[xi % 2]
                    k.dma("sp", f"xs{xi % 2}", [(xr[:], xsrc[r0 + HALO + blk * 128:r0 + HALO + (blk + 1) * 128, :])], reads=[xsrc], writes=[xr])
                    k.op("act", lambda e: e.activation(yb[:], stt[:, :], AF.Square, accum_out=st[:, 8:9]), reads=sbuf, writes=[yb, st])
                    k.op("dve", lambda e: e.tensor_scalar(st[:, 9:10], st[:, 8:9], 1.0 / D, EPS, op0=ALU.mult, op1=ALU.add), reads=[st], writes=[st])
                    k.op("act", lambda e: e.activation(st[:, 10:11], st[:, 9:10], AF.Sqrt), reads=[st], writes=[st])
                    k.op("dve", lambda e: e.reciprocal(st[:, 11:12], st[:, 10:11]), reads=[st], writes=[st])
                    k.op("dve", lambda e: e.scalar_tensor_tensor(yb[:], stt[:, :], st[:, 11:12], bc[:, B_POST:B_POST + D], op0=ALU.mult, op1=ALU.mult),
                         reads=sbuf + [st, bc], writes=[yb])
                    k.op("dve", lambda e: e.tensor_tensor(yb[:], yb[:], xr[:], op=ALU.add), reads=[yb, xr], writes=[yb])
                    row = i * 512 + blk * 128
                    if last:
                        k.dma("sp", "yout", [(y_d[row:row + 128, :], yb[:])], reads=[yb])
                    else:
                        pr = [(x1[HALO + row:HALO + row + 128, :], yb[:])]
                        if row == 0:
                            pr.append((hb_loc[0:HALO, :], yb[0:HALO, :]))
                        if row + 128 == TOWN:
                            pr.append((hb_loc[HALO:2 * HALO, :], yb[128 - HALO:128, :]))
                        k.dma("sp", "yout", pr, reads=[yb], writes=[x1, hb_loc])

        k.op("pool", lambda e: e.memset(VA[:], 1.0), writes=[VA])
        for l in range(depth):
            layer(l, xin if l == 0 else x1, l == depth - 1)
            if l < depth - 1:
                halo_exchange()

    try:
        body()
    except _Stop:
        pass

    outb = Buf(None)
    for nm in ("yout", "dbg", "xch"):
        if nm in k.dsems:
            outb.d.lw = (k.dsems[nm], k.dcnt[nm])
            k.wait_all("sp", [outb])
    return nc


def _rope_tables(pos):
    pos = np.asarray(pos)
    inv = (10000.0 ** (-np.arange(16, dtype=np.float32) / 16)).astype(np.float32)
    row = (pos // 64).astype(np.float32)
    col = (pos % 64).astype(np.float32)
    d = np.arange(128) % 64
    p = np.where((d < 32)[:, None], row[None, :], col[None, :]).astype(np.float32)
    f = inv[(d % 32) % 16][:, None]
    ang = (p * f).astype(np.float32)
    return np.stack([np.cos(ang), np.sin(ang)]).astype(np.float32)


def _consts():
    c = np.zeros((128, 4, 128), np.float32)
    c[:, 0, :] = np.eye(128)
    for kk in range(128):
        if kk % 32 >= 16:
            c[kk, 1, kk - 16] = -1.0
        else:
            c[kk, 1, kk + 16] = 1.0
    for kk in range(128):
        c[kk, 2, (kk // 64) * 64:(kk // 64) * 64 + 64] = 1.0 / 64
    c[:, 3, :] = 1.0 / 256
    return c.astype(ml_dtypes.bfloat16)


def _layer_params(l, pre_norm, post_norm, w_in, w_out, q_norm, k_norm, conv_dw, conv_dw_b,
                  conv_ln_g, conv_ln_b, sg_ln_g, sg_ln_b, sg_w, sg_b):
    perm = np.arange(512).reshape(8, 64)
    hp = np.concatenate([np.concatenate([perm[c], perm[4 + c]]) for c in range(4)])
    cols = np.concatenate([hp, 512 + np.arange(256), 768 + hp, np.arange(1280, DIN)])
    wi = np.ascontiguousarray(w_in[l][:, cols])
    rows = np.concatenate([hp, np.arange(512, 1024)])
    wo = np.ascontiguousarray(w_out[l][rows, :])
    prm = np.zeros((128, P_N), np.float32)
    prm[:, P_GPRE:P_GPRE + 8] = pre_norm[l].reshape(8, 128).T
    prm[:, P_GQ] = np.tile(q_norm[l], 2)
    prm[:, P_GK] = np.tile(k_norm[l], 2)
    prm[:, P_EPS] = EPS
    prm[:, P_CB:P_CB + 2] = conv_dw_b[l].reshape(2, 128).T
    prm[:, P_CLG:P_CLG + 2] = conv_ln_g[l].reshape(2, 128).T
    prm[:, P_CLB:P_CLB + 2] = conv_ln_b[l].reshape(2, 128).T
    prm[:, P_DW:P_DW + 62] = conv_dw[l].T.reshape(2, 128, 31).transpose(1, 0, 2).reshape(128, 62)
    bcv = np.zeros((128, B_N), np.float32)
    bcv[:, B_POST:B_POST + D] = post_norm[l][None, :]
    bcv[:, B_SLG:B_SLG + 256] = sg_ln_g[l][None, :]
    bcv[:, B_SLB:B_SLB + 256] = sg_ln_b[l][None, :]
    bcv[:, B_SGB:B_SGB + 4] = sg_b[l].T
    wst = np.ascontiguousarray(sg_w[l].transpose(2, 0, 1))
    return {"w_in": wi, "w_out": wo, "prm": prm, "bc": bcv, "wst": wst}


def _sel_matrix(r, rpg):
    m = np.zeros((128, 32), np.float32)
    if r > 0:
        for j in range(HALO):
            m[(r - 1) * 32 + HALO + j, j] = 1.0
    if r < rpg - 1:
        for j in range(HALO):
            m[(r + 1) * 32 + j, HALO + j] = 1.0
    return m


def run_fused(x, params, depth, trace=False):
    B, S, _ = x.shape
    per = NCORES // B
    TOWN = S // per
    nc = build_fused(TOWN, S, depth)
    lps = [_layer_params(l, **params) for l in range(depth)]
    stacked = {kk: np.ascontiguousarray(np.stack([lp[kk] for lp in lps])) for kk in lps[0]}
    cbf = _consts()
    in_maps = []
    for c in range(NCORES):
        b, r = divmod(c, per)
        t0 = r * TOWN
        xo = np.zeros((TOWN + 2 * HALO, D), np.float32)
        lo, hi = max(0, t0 - HALO), min(S, t0 + TOWN + HALO)
        xo[lo - (t0 - HALO):hi - (t0 - HALO)] = x[b, lo:hi]
        m = {"x_own": xo, "ropeq": _rope_tables(np.arange(t0, t0 + TOWN)), "cbf": cbf, "sel": _sel_matrix(r, per)}
        m.update(stacked)
        in_maps.append(m)
    res = run_bass_kernel_spmd(nc, in_maps, core_ids=list(range(NCORES)), trace=trace)
    y = np.empty_like(x)
    for c in range(NCORES):
        b, r = divmod(c, per)
        y[b, r * TOWN:(r + 1) * TOWN] = res.results[c]["y"]
    return y, res


def kernel(x, pre_norm, post_norm, w_in, w_out, q_norm, k_norm, conv_dw, conv_dw_b,
           conv_ln_g, conv_ln_b, sg_ln_g, sg_ln_b, sg_w, sg_b):
    params = dict(pre_norm=pre_norm, post_norm=post_norm, w_in=w_in, w_out=w_out, q_norm=q_norm, k_norm=k_norm,
                  conv_dw=conv_dw, conv_dw_b=conv_dw_b, conv_ln_g=conv_ln_g, conv_ln_b=conv_ln_b,
                  sg_ln_g=sg_ln_g, sg_ln_b=sg_ln_b, sg_w=sg_w, sg_b=sg_b)
    params = {kk: np.asarray(v, np.float32) for kk, v in params.items()}
    x = np.asarray(x, np.float32)
    y, _ = run_fused(x, params, int(pre_norm.shape[0]))
    return y
```
